# Optimizing a Trainium2 kernel written in Bass

```python
import math
import jax, jax.numpy as jnp
from jax import lax
import numpy as np

D_MODEL = 1024
BATCH = 8
SEQ = 8192
DEPTH = 4

N_EVEN = (DEPTH + 1) // 2
N_ODD = DEPTH // 2

MEM_LEN = 256
XA_HEADS = 4
XA_HEAD_DIM = D_MODEL // XA_HEADS

POOL_W = D_MODEL // 2
POOL_WINDOWS = (2, 4, 8, 16)
N_POOL_GROUPS = len(POOL_WINDOWS)
POOL_GROUP = POOL_W // N_POOL_GROUPS
CONV_W = D_MODEL // 2
CONV_K = 31

MLA_HEADS = 16
QK_NOPE = 64
QK_ROPE = 32
V_HEAD = 64
Q_LORA = 384
KV_LORA = 256
ROPE_THETA = 10000.0
Q_BLOCK = 128
MLA_SCALE = 1.0 / math.sqrt(QK_NOPE + QK_ROPE)

D_FF = 2816
FFN_CONV_K = 3

EPS = 1e-6
NEG = -1e30

kernel_name = "hybrid_pool_conv_mla_memxattn_convffn"


def rmsnorm(x, g):
    xf = x.astype(jnp.float32)
    y = xf * lax.rsqrt(jnp.mean(xf * xf, axis=-1, keepdims=True) + EPS)
    return (y * g.astype(jnp.float32)).astype(x.dtype)


def layernorm(x, g, b):
    xf = x.astype(jnp.float32)
    mu = jnp.mean(xf, axis=-1, keepdims=True)
    xc = xf - mu
    y = xc * lax.rsqrt(jnp.mean(xc * xc, axis=-1, keepdims=True) + EPS)
    return (y * g.astype(jnp.float32) + b.astype(jnp.float32)).astype(x.dtype)


def causal_dwconv(u, w):
    k = w.shape[0]
    return lax.conv_general_dilated(
        u, w[:, None, :], window_strides=(1,), padding=[(k - 1, 0)],
        dimension_numbers=("NWC", "WIO", "NWC"), feature_group_count=u.shape[-1])


def window_mean_minus_self(u, w):
    t = u.shape[1]
    uf = u.astype(jnp.float32)
    cs = jnp.cumsum(uf, axis=1)
    cs_lag = jnp.pad(cs, ((0, 0), (w, 0), (0, 0)))[:, :t]
    cnt = jnp.minimum(jnp.arange(t) + 1, w).astype(jnp.float32)
    return ((cs - cs_lag) / cnt[None, :, None] - uf).astype(u.dtype)


def rope_tables(positions):
    inv = 1.0 / (ROPE_THETA ** (jnp.arange(0, QK_ROPE, 2, dtype=jnp.float32) / QK_ROPE))
    ang = positions.astype(jnp.float32)[..., None] * inv
    return jnp.cos(ang), jnp.sin(ang)


def apply_rope(x, cos, sin):
    half = x.shape[-1] // 2
    c = cos.astype(x.dtype)
    s = sin.astype(x.dtype)
    x1, x2 = x[..., :half], x[..., half:]
    return jnp.concatenate([x1 * c - x2 * s, x1 * s + x2 * c], axis=-1)


def pool_conv_mixer(h, w_in, pool_w, pool_scale, dw_w, dw_b, ln_g, ln_b, w_out):
    b, t, _ = h.shape
    z = h @ w_in
    u, glu_a, glu_b = jnp.split(z, [POOL_W, POOL_W + CONV_W], axis=-1)
    ug = u.reshape(b, t, N_POOL_GROUPS, POOL_GROUP)
    pooled = jnp.stack([window_mean_minus_self(ug[:, :, i], w)
                        for i, w in enumerate(POOL_WINDOWS)], axis=2)
    ya = jnp.einsum("btgc,gcd->btgd", pooled, pool_w).reshape(b, t, POOL_W) * pool_scale
    gl = glu_a * jax.nn.sigmoid(glu_b)
    cv = causal_dwconv(gl, dw_w) + dw_b
    yb = jax.nn.silu(layernorm(cv, ln_g, ln_b))
    return jnp.concatenate([ya, yb], axis=-1) @ w_out


def mla_attention(h, cos, sin, w_dq_dkv, q_norm_g, w_uq, kv_norm_g, w_ukv, w_o):
    b, t, _ = h.shape
    c = h @ w_dq_dkv
    cq, ckv, k_pe = jnp.split(c, [Q_LORA, Q_LORA + KV_LORA], axis=-1)
    q = (rmsnorm(cq, q_norm_g) @ w_uq).reshape(b, t, MLA_HEADS, QK_NOPE + QK_ROPE)
    q_nope = q[..., :QK_NOPE]
    q_pe = apply_rope(q[..., QK_NOPE:], cos[:, :, None, :], sin[:, :, None, :])
    kv = (rmsnorm(ckv, kv_norm_g) @ w_ukv).reshape(b, t, MLA_HEADS, QK_NOPE + V_HEAD)
    k_nope, v = kv[..., :QK_NOPE], kv[..., QK_NOPE:]
    k_pe = apply_rope(k_pe, cos, sin)
    nb = t // Q_BLOCK
    kpos = jnp.arange(t)

    def block(args):
        qn, qp, i = args
        s = (jnp.einsum("bqhd,bkhd->bhqk", qn, k_nope)
             + jnp.einsum("bqhr,bkr->bhqk", qp, k_pe)).astype(jnp.float32) * MLA_SCALE
        qpos = i * Q_BLOCK + jnp.arange(Q_BLOCK)
        s = jnp.where(kpos[None, :] <= qpos[:, None], s, NEG)
        p = jax.nn.softmax(s, axis=-1).astype(v.dtype)
        return jnp.einsum("bhqk,bkhd->bqhd", p, v)

    qn_b = q_nope.reshape(b, nb, Q_BLOCK, MLA_HEADS, QK_NOPE).transpose(1, 0, 2, 3, 4)
    qp_b = q_pe.reshape(b, nb, Q_BLOCK, MLA_HEADS, QK_ROPE).transpose(1, 0, 2, 3, 4)
    o = lax.map(block, (qn_b, qp_b, jnp.arange(nb)))
    o = o.transpose(1, 0, 2, 3, 4).reshape(b, t, MLA_HEADS * V_HEAD)
    return o @ w_o


def memory_cross_attention(h, m, wq, wkv, wo):
    b, t, _ = h.shape
    q = (h @ wq).reshape(b, t, XA_HEADS, XA_HEAD_DIM)
    k, v = jnp.split(m @ wkv, 2, axis=-1)
    k = k.reshape(b, MEM_LEN, XA_HEADS, XA_HEAD_DIM)
    v = v.reshape(b, MEM_LEN, XA_HEADS, XA_HEAD_DIM)
    s = jnp.einsum("bthd,bmhd->bhtm", q, k).astype(jnp.float32) * (XA_HEAD_DIM ** -0.5)
    p = jax.nn.softmax(s, axis=-1).astype(v.dtype)
    o = jnp.einsum("bhtm,bmhd->bthd", p, v).reshape(b, t, D_MODEL)
    return o @ wo


def conv_ffn(h, w_up, conv_w, conv_b, w_down):
    a, g = jnp.split(h @ w_up, 2, axis=-1)
    g = causal_dwconv(g, conv_w) + conv_b
    return (jax.nn.silu(g) * a) @ w_down


def setup_inputs(seed: int = 0) -> dict:
    key = jax.random.key(seed)
    ks = iter(jax.random.split(key, 40))
    f32 = jnp.float32

    def dense(shape, fan_in, scale=1.0):
        return jax.random.normal(next(ks), shape, f32) * (scale * fan_in ** -0.5)

    def gain(shape):
        return 1.0 + 0.02 * jax.random.normal(next(ks), shape, f32)

    def bias(shape):
        return 0.01 * jax.random.normal(next(ks), shape, f32)

    out_scale = 0.5
    x = jax.random.normal(next(ks), (BATCH, SEQ, D_MODEL), f32)
    mem = jax.random.normal(next(ks), (BATCH, MEM_LEN, D_MODEL), f32)
    offsets = jax.random.randint(next(ks), (BATCH, 1), 0, 4096, dtype=jnp.int32)
    positions = offsets + jnp.arange(SEQ, dtype=jnp.int32)[None, :]
    return {
        "x": x,
        "mem": mem,
        "positions": positions,
        "norm_mix_g": gain((DEPTH, D_MODEL)),
        "norm_xa_g": gain((DEPTH, D_MODEL)),
        "norm_mem_g": gain((DEPTH, D_MODEL)),
        "xa_wq": dense((DEPTH, D_MODEL, D_MODEL), D_MODEL),
        "xa_wkv": dense((DEPTH, D_MODEL, 2 * D_MODEL), D_MODEL),
        "xa_wo": dense((DEPTH, D_MODEL, D_MODEL), D_MODEL, out_scale),
        "norm_ffn_g": gain((DEPTH, D_MODEL)),
        "ffn_w_up": dense((DEPTH, D_MODEL, 2 * D_FF), D_MODEL),
        "ffn_conv_w": dense((DEPTH, FFN_CONV_K, D_FF), FFN_CONV_K),
        "ffn_conv_b": bias((DEPTH, D_FF)),
        "ffn_w_down": dense((DEPTH, D_FF, D_MODEL), D_FF, out_scale),
        "pc_w_in": dense((N_EVEN, D_MODEL, POOL_W + 2 * CONV_W), D_MODEL),
        "pool_w": dense((N_EVEN, N_POOL_GROUPS, POOL_GROUP, POOL_GROUP), POOL_GROUP),
        "pool_scale": gain((N_EVEN, POOL_W)),
        "conv_dw_w": dense((N_EVEN, CONV_K, CONV_W), CONV_K),
        "conv_dw_b": bias((N_EVEN, CONV_W)),
        "conv_ln_g": gain((N_EVEN, CONV_W)),
        "conv_ln_b": bias((N_EVEN, CONV_W)),
        "pc_w_out": dense((N_EVEN, POOL_W + CONV_W, D_MODEL), POOL_W + CONV_W, out_scale),
        "mla_w_dq_dkv": dense((N_ODD, D_MODEL, Q_LORA + KV_LORA + QK_ROPE), D_MODEL),
        "mla_q_norm_g": gain((N_ODD, Q_LORA)),
        "mla_w_uq": dense((N_ODD, Q_LORA, MLA_HEADS * (QK_NOPE + QK_ROPE)), Q_LORA),
        "mla_kv_norm_g": gain((N_ODD, KV_LORA)),
        "mla_w_ukv": dense((N_ODD, KV_LORA, MLA_HEADS * (QK_NOPE + V_HEAD)), KV_LORA),
        "mla_w_o": dense((N_ODD, MLA_HEADS * V_HEAD, D_MODEL), MLA_HEADS * V_HEAD, out_scale),
        "final_norm_g": gain((D_MODEL,)),
    }


def reference(x, mem, positions, norm_mix_g, norm_xa_g, norm_mem_g, xa_wq, xa_wkv, xa_wo,
              norm_ffn_g, ffn_w_up, ffn_conv_w, ffn_conv_b, ffn_w_down,
              pc_w_in, pool_w, pool_scale, conv_dw_w, conv_dw_b, conv_ln_g, conv_ln_b, pc_w_out,
              mla_w_dq_dkv, mla_q_norm_g, mla_w_uq, mla_kv_norm_g, mla_w_ukv, mla_w_o,
              final_norm_g):
    cos, sin = rope_tables(positions)
    for l in range(DEPTH):
        h = rmsnorm(x, norm_mix_g[l])
        if l % 2 == 0:
            e = l // 2
            x = x + pool_conv_mixer(h, pc_w_in[e], pool_w[e], pool_scale[e], conv_dw_w[e],
                                    conv_dw_b[e], conv_ln_g[e], conv_ln_b[e], pc_w_out[e])
        else:
            o = l // 2
            x = x + mla_attention(h, cos, sin, mla_w_dq_dkv[o], mla_q_norm_g[o], mla_w_uq[o],
                                  mla_kv_norm_g[o], mla_w_ukv[o], mla_w_o[o])
        x = x + memory_cross_attention(rmsnorm(x, norm_xa_g[l]), rmsnorm(mem, norm_mem_g[l]),
                                       xa_wq[l], xa_wkv[l], xa_wo[l])
        x = x + conv_ffn(rmsnorm(x, norm_ffn_g[l]), ffn_w_up[l], ffn_conv_w[l], ffn_conv_b[l],
                         ffn_w_down[l])
    return rmsnorm(x, final_norm_g)
```

```python
import contextlib
import numpy as np
import concourse.bass as bass
import concourse.mybir as mybir
from concourse.bass_utils import run_bass_kernel_spmd

F32 = mybir.dt.float32
BF16 = mybir.dt.bfloat16
I32 = mybir.dt.int32
AF = mybir.ActivationFunctionType
ALU = mybir.AluOpType

D = 1024
KC = 8
TT = 512
DFF = 2816
NJ = 22
NJH = 11
EPS = 1e-6
MEM = 256
SLAB = 512
NSLAB = 72
SELF_SYNC = True
NDSEM = 56
NODEFER = False
OBANK = (5, 7)
BCBANK = 6
SELF_DIST = 2

VOFF = {}
_nv = 0


def _valloc(name, n):
    global _nv
    VOFF[name] = _nv
    _nv += n


for _l in range(4):
    _valloc("g_mix%d" % _l, 8)
    _valloc("g_xa%d" % _l, 8)
    _valloc("g_mem%d" % _l, 8)
    _valloc("g_ffn%d" % _l, 8)
    _valloc("ffn_cw%d" % _l, NJ * 4)
for _e in range(2):
    _valloc("pool_scale%d" % _e, 4)
    _valloc("dw_w%d" % _e, 4 * 31)
    _valloc("dw_b%d" % _e, 4)
    _valloc("ln_g%d" % _e, 4)
    _valloc("ln_b%d" % _e, 4)
    _valloc("g_q%d" % _e, 3)
    _valloc("g_kv%d" % _e, 2)
_valloc("g_final", 8)
_valloc("inv128", 1)
_valloc("sgn128", 1)
_valloc("invcnt", 4 * 16)
NV = _nv


class Buf:
    __slots__ = ("name", "t", "last_w", "reads", "dsem", "dcnt")

    def __init__(self, name, t=None):
        self.name = name
        self.t = t
        self.last_w = None
        self.reads = {}
        self.dsem = None
        self.dcnt = 0

    def __getitem__(self, idx):
        return self.t[idx]


class PBuf(Buf):
    __slots__ = ("off",)

    def __init__(self, name, t, off):
        Buf.__init__(self, name, t)
        self.off = off

    def __getitem__(self, idx):
        if not isinstance(idx, tuple):
            idx = (idx, slice(None))
        ps, cs = idx
        a = cs.start or 0
        b = 512 if cs.stop is None else cs.stop
        return self.t[ps, self.off + a:self.off + b]


class KB:
    def __init__(self, nc):
        self.nc = nc
        self.E = {"pe": nc.tensor, "act": nc.scalar, "dve": nc.vector, "pool": nc.gpsimd, "sp": nc.sync}
        self.semh = {k: nc.alloc_semaphore("e_" + k) for k in self.E}
        self.cnt = {k: 0 for k in self.E}
        self.known = {k: {} for k in self.E}
        self.dsems = {}
        self.free_dsems = []
        self.ndsem = 0
        for _ in range(NDSEM):
            key = "d%d" % self.ndsem
            self.ndsem += 1
            self.semh[key] = nc.alloc_semaphore(key)
            self.dsems[key] = 0
            self.free_dsems.append(key)
        for key in self.semh:
            nc.gpsimd.sem_clear(self.semh[key])
        nc.all_engine_barrier()
        self.stack = None
        self.local = []
        self.phase_no = 0

    def sb(self, name, shape, dt, persistent=False):
        if persistent or self.stack is None:
            b = Buf(name, self.nc.alloc_sbuf_tensor(name, shape, dt))
        else:
            name = "%s_p%d" % (name, self.phase_no)
            t = self.stack.enter_context(self.nc.sbuf_tensor(name, shape, dt))
            b = Buf(name, t)
            self.local.append(b)
        return b

    def dram(self, name):
        return Buf(name, None)

    def _dsem(self, b):
        if b.dsem is None:
            if self.free_dsems:
                key = self.free_dsems.pop()
            else:
                key = "d%d" % self.ndsem
                self.ndsem += 1
                self.semh[key] = self.nc.alloc_semaphore(key)
                self.dsems[key] = 0
            b.dsem = key
            b.dcnt = self.dsems[key]
        return b.dsem

    def _wait(self, eng, deps):
        kn = self.known[eng]
        for (s, v) in deps:
            if s == eng and not (SELF_SYNC and eng in ("dve", "act") and v > self.cnt[eng] - SELF_DIST):
                continue
            if kn.get(s, 0) >= v:
                continue
            self.E[eng].wait_ge(self.semh[s], v)
            kn[s] = v

    @staticmethod
    def _deps(reads, writes, skip_sem=None):
        deps = []
        for b in reads:
            if b.last_w is not None:
                deps.append(b.last_w)
        for b in writes:
            if b.last_w is not None and b.last_w[0] != skip_sem:
                deps.append(b.last_w)
            deps.extend(b.reads.items())
        return deps

    def op(self, eng, fn, reads=(), writes=(), sig=True):
        self._wait(eng, self._deps(reads, writes))
        ins = fn(self.E[eng])
        if sig:
            self.cnt[eng] += 1
            ins.then_inc(self.semh[eng], 1)
            v = self.cnt[eng]
        else:
            v = self.cnt[eng] + 1
        for b in reads:
            if b.reads.get(eng, 0) < v:
                b.reads[eng] = v
        for b in writes:
            b.last_w = (eng, v)
            b.reads = {}
        return ins

    def dma(self, q, out, in_, reads=(), writes=(), sembuf=None):
        key = self._dsem(sembuf)
        self._wait(q, self._deps(reads, writes, skip_sem=key))
        ins = self.E[q].dma_start(out=out, in_=in_)
        self.dsems[key] += 16
        sembuf.dcnt = self.dsems[key]
        ins.then_inc(self.semh[key], 16)
        ev = (key, self.dsems[key])
        for b in reads:
            if b.reads.get(key, 0) < ev[1]:
                b.reads[key] = ev[1]
        for b in writes:
            b.last_w = ev
            b.reads = {}
        return ins

    def barrier(self, engines=("pe", "act", "dve", "sp")):
        evs = [(e, self.cnt[e]) for e in ("pe", "act", "dve") if self.cnt[e] > 0]
        for b in self.local:
            if b.dsem is not None:
                evs.append((b.dsem, self.dsems[b.dsem]))
        for e in engines:
            self._wait(e, evs)

    def begin_phase(self):
        self.stack = contextlib.ExitStack()
        self.local = []
        self.phase_no += 1

    def end_phase(self):
        self.barrier()
        for b in self.local:
            if b.dsem is not None:
                self.free_dsems.append(b.dsem)
        self.stack.close()
        self.stack = None
        self.local = []


class Prog:
    def __init__(self, NT, phases, debug=False):
        self.debug = debug
        self.NT = NT
        self.T = NT * TT
        T = self.T
        nc = bass.Bass("TRN2", target_bir_lowering=False)
        self.nc = nc
        self.k = KB(nc)
        k = self.k
        self.xin = nc.dram_tensor("xT", [D, T], F32, kind="ExternalInput").ap()
        self.memT = nc.dram_tensor("memT", [D, MEM], F32, kind="ExternalInput").ap()
        self.pos = nc.dram_tensor("pos", [128, T], I32, kind="ExternalInput").ap()
        self.vecs_d = nc.dram_tensor("vecs", [128, NV], F32, kind="ExternalInput").ap()
        self.wffn = nc.dram_tensor("wffn", [4, 2, 128, NJH * 3072], F32, kind="ExternalInput").ap()
        self.wxa = nc.dram_tensor("wxa", [4, 128, 16384], F32, kind="ExternalInput").ap()
        self.wkv = nc.dram_tensor("wkv", [4, 128, 16384], F32, kind="ExternalInput").ap()
        self.wmixe = nc.dram_tensor("wmixe", [2, 128, 20992], F32, kind="ExternalInput").ap()
        self.wmla = nc.dram_tensor("wmla", [2, 128, 27136], F32, kind="ExternalInput").ap()
        self.consts_d = nc.dram_tensor("cbf", [128, 1024], F32, kind="ExternalInput").ap()
        self.yout = nc.dram_tensor("yT", [D, T], F32, kind="ExternalOutput").ap()
        self.xres = nc.dram_tensor("xres", [D, T], F32, kind="Internal").ap()
        self.hT = nc.dram_tensor("hTs", [D, T], BF16, kind="Internal").ap()
        self.kvs = nc.dram_tensor("kvs", [4, 128, 4096], BF16, kind="Internal").ap()
        self.qT = nc.dram_tensor("qTs", [16, 96, T], BF16, kind="Internal").ap()
        self.kT = nc.dram_tensor("kTs", [16, 96, T], BF16, kind="Internal").ap()
        self.vS = nc.dram_tensor("vSs", [T, 16 * 65], BF16, kind="Internal").ap()
        self.oT = nc.dram_tensor("oTs", [D, T], BF16, kind="Internal").ap()
        self.ropeC = nc.dram_tensor("ropeC", [128, T], F32, kind="Internal").ap()
        self.ropeS = nc.dram_tensor("ropeS", [128, T], F32, kind="Internal").ap()
        self.d_x = [k.dram("dx%d" % t) for t in range(NT)]
        self.d_h = [k.dram("dh%d" % t) for t in range(NT)]
        self.d_o = [k.dram("do%d" % t) for t in range(NT)]
        self.d_q = [k.dram("dq%d" % t) for t in range(NT)]
        self.d_k = [k.dram("dk%d" % t) for t in range(NT)]
        self.d_v = [k.dram("dv%d" % t) for t in range(NT)]
        self.d_rope = [k.dram("dr%d" % t) for t in range(NT)]
        self.d_kvs = [k.dram("dkvs%d" % l) for l in range(4)]
        self.d_y = k.dram("dy")
        self.W = nc.alloc_sbuf_tensor("Warena", [128, NSLAB * SLAB], BF16)
        self.slabs = [Buf("slab%d" % i, self.W) for i in range(NSLAB)]
        self.vecs = k.sb("s_vecs", [128, NV], F32, persistent=True)
        self.cbf = k.sb("s_cbf", [128, 1024], BF16, persistent=True)
        self.epsb = k.sb("epsb", [128, 1], F32, persistent=True)
        self.pst = nc.alloc_psum_tensor("pst", [128, 4096], F32)
        self.PS = [PBuf("ps%d" % i, self.pst, i * 512) for i in range(8)]
        self.ones32 = k.sb("s_ones32", [128, 128], F32, persistent=True)
        k.op("dve", lambda e: e.memset(self.ones32[:], 1.0), writes=[self.ones32])
        self.psi = 0
        k.dma("sp", self.vecs[:], self.vecs_d[:, :], writes=[self.vecs], sembuf=self.vecs)
        k.dma("pool", self.cbf[:], self.consts_d[:, :], writes=[self.cbf], sembuf=self.cbf)
        k.op("dve", lambda e: e.memset(self.epsb[:], EPS), writes=[self.epsb])
        self.first_x = True
        self.phases = phases
        for pi, ph in enumerate(phases):
            self.last = (pi == len(phases) - 1)
            getattr(self, "ph_" + ph[0])(*ph[1:])
        k._wait("sp", [(s, v) for (s, v) in ([self.d_y.last_w] if self.d_y.last_w else [])])
        k.barrier(engines=("sp",))
        evs = [(key, v) for key, v in k.dsems.items() if v > 0]
        k._wait("sp", evs)

    def ps(self):
        b = self.PS[self.psi]
        self.psi = (self.psi + 1) % 8
        return b

    def wv(self, c0, n):
        return self.W[:, c0:c0 + n], self.slabs[c0 // SLAB:(c0 + n - 1) // SLAB + 1]

    def wload(self, src, c0, n, piece=3072):
        assert c0 % SLAB == 0 and n % SLAB == 0
        o = 0
        while o < n:
            m = min(piece, n - o)
            sl = self.slabs[(c0 + o) // SLAB:(c0 + o + m) // SLAB]
            self.k.dma("pool", self.W[:, c0 + o:c0 + o + m], src[:, o:o + m], writes=sl, sembuf=sl[0])
            o += m

    def vcol(self, name, i=0):
        o = VOFF[name] + i
        return self.vecs[:, o:o + 1]

    def xsrc(self):
        return self.xin if self.first_x else self.xres

    def xtile_ap(self, base, t):
        return base.rearrange("(c p) t -> p c t", p=128)[:, :, t * TT:(t + 1) * TT]

    def load_x(self, xs, t):
        k = self.k
        rd = [] if self.first_x else [self.d_x[t]]
        k.dma("sp", xs[:], self.xtile_ap(self.xsrc(), t), reads=rd, writes=[xs], sembuf=xs)

    def store_x(self, xs, t):
        k = self.k
        if self.last:
            k.dma("sp", self.xtile_ap(self.yout, t), xs[:], reads=[xs], writes=[self.d_y], sembuf=xs)
        else:
            k.dma("sp", self.xtile_ap(self.xres, t), xs[:], reads=[xs], writes=[self.d_x[t]], sembuf=xs)

    def rmsnorm(self, xs, nch, N, gname, h, sq, rstd, tmp, dim):
        k = self.k
        ones = self.cbf[:, 0:128]
        k.op("act", lambda e: e.activation(out=sq[:, 0:nch, 0:N], in_=xs[:, 0:nch, 0:N], func=AF.Square),
             reads=[xs], writes=[sq])
        p = self.ps()
        for c in range(nch):
            k.op("pe", lambda e: e.matmul(p[:, 0:N], ones, sq[:, c, 0:N], start=(c == 0), stop=(c == nch - 1)),
                 reads=[sq, self.cbf], writes=[p], sig=(c == nch - 1))
        k.op("act", lambda e: e.activation(out=tmp[:, 0:N], in_=p[:, 0:N], func=AF.Sqrt, scale=1.0 / dim,
                                           bias=self.epsb[:, 0:1]), reads=[p, self.epsb], writes=[tmp])
        k.op("dve", lambda e: e.reciprocal(rstd[:, 0:N], tmp[:, 0:N]), reads=[tmp], writes=[rstd])
        for c in range(nch):
            k.op("dve", lambda e: e.scalar_tensor_tensor(h[:, c, 0:N], xs[:, c, 0:N], self.vcol(gname, c),
                                                         rstd[:, 0:N], ALU.mult, ALU.mult),
                 reads=[xs, rstd, self.vecs], writes=[h])

    def ph_ffn(self, l, s):
        k = self.k
        NT = self.NT
        k.begin_phase()
        self.wload(self.wffn[l, s], 0, NJH * 3072)
        xs = [k.sb("f_xs%d" % i, [128, KC, TT], F32) for i in range(3)]
        h = [k.sb("f_h%d" % i, [128, KC, TT], BF16) for i in range(2)]
        m = [k.sb("f_m%d" % i, [128, NJH, TT], BF16) for i in range(2)]
        G = [k.sb("f_G%d" % j, [128, TT + 2], F32) for j in range(NJH)]
        rstd = k.sb("f_rstd", [128, TT], F32)
        tmp = k.sb("f_tmp", [128, TT], F32)
        cA = [k.sb("f_cA%d" % i, [128, TT], F32) for i in range(2)]
        cB = [k.sb("f_cB%d" % i, [128, TT], F32) for i in range(2)]
        for j in range(NJH):
            k.op("dve", lambda e: e.memset(G[j][:, 0:2], 0.0), writes=[G[j]])

        def load(t):
            self.load_x(xs[t % 3], t)
            if s == 1:
                k.dma("sp", h[t % 2][:], self.xtile_ap(self.hT, t), reads=[self.d_h[t]], writes=[h[t % 2]],
                      sembuf=h[t % 2])

        def up(t):
            x_ = xs[t % 3]
            h_ = h[t % 2]
            m_ = m[t % 2]
            if s == 0:
                self.rmsnorm(x_, KC, TT, "g_ffn%d" % l, h_, m_, rstd, tmp, D)
                k.dma("sp", self.xtile_ap(self.hT, t), h_[:], reads=[h_], writes=[self.d_h[t]], sembuf=h_)
            for jj in range(NJH):
                j = s * NJH + jj
                c0 = jj * 3072
                pa = self.ps()
                pg = self.ps()
                for (pp, off) in ((pa, 0), (pg, 128)):
                    for kc in range(KC):
                        wap, wsl = self.wv(c0 + kc * 256 + off, 128)
                        k.op("pe", lambda e: e.matmul(pp[:], wap, h_[:, kc, :], start=(kc == 0), stop=(kc == KC - 1)),
                             reads=[h_] + wsl, writes=[pp], sig=(kc == KC - 1))
                g_ = G[jj]
                if t > 0:
                    k.op("dve", lambda e: e.tensor_copy(out=g_[:, 0:2], in_=g_[:, TT:TT + 2]), reads=[g_], writes=[g_])
                k.op("act", lambda e: e.activation(out=g_[:, 2:TT + 2], in_=pg[:], func=AF.Copy), reads=[pg], writes=[g_])
                ca = cA[jj % 2]
                cb = cB[jj % 2]
                cw = lambda i: self.vcol("ffn_cw%d" % l, j * 4 + i)
                k.op("act", lambda e: e.activation(out=ca[:], in_=pg[:], func=AF.Identity, scale=cw(2), bias=cw(3)),
                     reads=[pg, self.vecs], writes=[ca])
                k.op("dve", lambda e: e.scalar_tensor_tensor(cb[:], g_[:, 1:TT + 1], cw(1), ca[:], ALU.mult, ALU.add),
                     reads=[g_, ca, self.vecs], writes=[cb])
                k.op("dve", lambda e: e.scalar_tensor_tensor(ca[:], g_[:, 0:TT], cw(0), cb[:], ALU.mult, ALU.add),
                     reads=[g_, cb, self.vecs], writes=[ca])
                k.op("act", lambda e: e.activation(out=cb[:], in_=ca[:], func=AF.Silu), reads=[ca], writes=[cb])
                k.op("dve", lambda e: e.tensor_tensor(m_[:, jj, :], cb[:], pa[:], ALU.mult), reads=[cb, pa], writes=[m_])

        def down(t):
            x_ = xs[t % 3]
            m_ = m[t % 2]
            for c in range(KC):
                p = self.ps()
                for jj in range(NJH):
                    wap, wsl = self.wv(jj * 3072 + 2048 + c * 128, 128)
                    k.op("pe", lambda e: e.matmul(p[:], wap, m_[:, jj, :], start=(jj == 0), stop=(jj == NJH - 1)),
                         reads=[m_] + wsl, writes=[p], sig=(jj == NJH - 1))
                k.op("dve", lambda e: e.tensor_tensor(x_[:, c, :], x_[:, c, :], p[:], ALU.add), reads=[x_, p], writes=[x_])
            self.store_x(x_, t)

        load(0)
        if NT > 1:
            load(1)
        for t in range(NT + 1):
            if t < NT:
                up(t)
            if t > 0:
                down(t - 1)
            if t + 2 < NT + 1 and t + 2 < NT:
                load(t + 2)
        if s == 1 or True:
            self.first_x = False
        k.end_phase()

    def ph_memkv(self, l):
        k = self.k
        k.begin_phase()
        self.wload(self.wkv[l], 0, 16384, piece=4096)
        ms = k.sb("mk_ms", [128, KC, MEM], F32)
        mn = k.sb("mk_mn", [128, KC, MEM], BF16)
        sq = k.sb("mk_sq", [128, KC, MEM], BF16)
        rstd = k.sb("mk_rstd", [128, TT], F32)
        tmp = k.sb("mk_tmp", [128, TT], F32)
        kv = k.sb("mk_kv", [128, 4096], BF16)
        k.dma("sp", ms[:], self.memT.rearrange("(c p) m -> p c m", p=128), writes=[ms], sembuf=ms)
        self.rmsnorm(ms, KC, MEM, "g_mem%d" % l, mn, sq, rstd, tmp, D)
        for c in range(KC):
            p = self.ps()
            for kc in range(KC):
                wap, wsl = self.wv(kc * 2048 + c * 128, 128)
                k.op("pe", lambda e: e.matmul(p[:, 0:MEM], wap, mn[:, kc, :], start=(kc == 0), stop=(kc == KC - 1)),
                     reads=[mn] + wsl, writes=[p], sig=(kc == KC - 1))
            k.op("act", lambda e: e.activation(out=kv[:, c * MEM:(c + 1) * MEM], in_=p[:, 0:MEM], func=AF.Copy,
                                               scale=1.0 / 16.0), reads=[p], writes=[kv])
        for mc in range(2):
            for hf in range(2):
                p = self.ps()
                for kc in range(KC):
                    wap, wsl = self.wv(kc * 2048 + 1024 + hf * 512, 512)
                    k.op("pe", lambda e: e.matmul(p[:], mn[:, kc, mc * 128:(mc + 1) * 128], wap, start=(kc == 0),
                                                  stop=(kc == KC - 1)), reads=[mn] + wsl, writes=[p], sig=(kc == KC - 1))
                o = 2048 + mc * 1024 + hf * 512
                k.op("dve", lambda e: e.tensor_copy(out=kv[:, o:o + 512], in_=p[:]), reads=[p], writes=[kv])
        k.dma("sp", self.kvs[l], kv[:], reads=[kv], writes=[self.d_kvs[l]], sembuf=kv)
        k.end_phase()

    def ph_xa(self, l):
        k = self.k
        NT = self.NT
        k.begin_phase()
        self.wload(self.wxa[l], 0, 16384, piece=4096)
        kv = k.sb("xa_kv", [128, 4096], BF16)
        k.dma("sp", kv[:], self.kvs[l], reads=[self.d_kvs[l]], writes=[kv], sembuf=kv)
        xs = [k.sb("xa_xs%d" % i, [128, KC, TT], F32) for i in range(2)]
        h = k.sb("xa_h", [128, KC, TT], BF16)
        sq = k.sb("xa_sq", [128, KC, TT], BF16)
        q = k.sb("xa_q", [128, KC, TT], BF16)
        o = k.sb("xa_o", [128, KC, TT], BF16)
        P = [k.sb("xa_P%d" % i, [128, 2, TT], BF16) for i in range(2)]
        rden = [k.sb("xa_rden%d" % i, [128, TT], F32) for i in range(2)]
        rstd = k.sb("xa_rstd", [128, TT], F32)
        tmp = k.sb("xa_tmp", [128, TT], F32)
        ones = self.cbf[:, 0:128]
        self.load_x(xs[0], 0)
        for t in range(NT):
            x_ = xs[t % 2]
            if t + 1 < NT:
                self.load_x(xs[(t + 1) % 2], t + 1)
            self.rmsnorm(x_, KC, TT, "g_xa%d" % l, h, sq, rstd, tmp, D)
            for c in range(KC):
                p = self.ps()
                for kc in range(KC):
                    wap, wsl = self.wv(kc * 1024 + c * 128, 128)
                    k.op("pe", lambda e: e.matmul(p[:], wap, h[:, kc, :], start=(kc == 0), stop=(kc == KC - 1)),
                         reads=[h] + wsl, writes=[p], sig=(kc == KC - 1))
                if c % 2 == 0:
                    k.op("act", lambda e: e.activation(out=q[:, c, :], in_=p[:], func=AF.Copy), reads=[p], writes=[q])
                else:
                    k.op("dve", lambda e: e.tensor_copy(out=q[:, c, :], in_=p[:]), reads=[p], writes=[q])
            for hd in range(4):
                P_ = P[hd % 2]
                rd = rden[hd % 2]
                for mc in range(2):
                    p = self.ps()
                    for dc in range(2):
                        c0 = (2 * hd + dc) * MEM + mc * 128
                        k.op("pe", lambda e: e.matmul(p[:], kv[:, c0:c0 + 128], q[:, 2 * hd + dc, :], start=(dc == 0),
                                                      stop=(dc == 1)), reads=[kv, q], writes=[p], sig=(dc == 1))
                    k.op("act", lambda e: e.activation(out=P_[:, mc, :], in_=p[:], func=AF.Exp), reads=[p], writes=[P_])
                pd = self.ps()
                for mc in range(2):
                    k.op("pe", lambda e: e.matmul(pd[:], ones, P_[:, mc, :], start=(mc == 0), stop=(mc == 1)),
                         reads=[P_, self.cbf], writes=[pd], sig=(mc == 1))
                k.op("dve", lambda e: e.reciprocal(rd[:], pd[:]), reads=[pd], writes=[rd])
                for dc in range(2):
                    p = self.ps()
                    for mc in range(2):
                        c0 = 2048 + mc * 1024 + hd * 256 + dc * 128
                        k.op("pe", lambda e: e.matmul(p[:], kv[:, c0:c0 + 128], P_[:, mc, :], start=(mc == 0),
                                                      stop=(mc == 1)), reads=[kv, P_], writes=[p], sig=(mc == 1))
                    k.op("dve", lambda e: e.tensor_tensor(o[:, 2 * hd + dc, :], p[:], rd[:], ALU.mult),
                         reads=[p, rd], writes=[o])
            for c in range(KC):
                p = self.ps()
                for kc in range(KC):
                    wap, wsl = self.wv(8192 + kc * 1024 + c * 128, 128)
                    k.op("pe", lambda e: e.matmul(p[:], wap, o[:, kc, :], start=(kc == 0), stop=(kc == KC - 1)),
                         reads=[o] + wsl, writes=[p], sig=(kc == KC - 1))
                k.op("dve", lambda e: e.tensor_tensor(x_[:, c, :], x_[:, c, :], p[:], ALU.add), reads=[x_, p], writes=[x_])
            self.store_x(x_, t)
        self.first_x = False
        k.end_phase()

    def ph_mixe(self, l):
        k = self.k
        NT = self.NT
        e = l // 2
        k.begin_phase()
        WIN, WPOOL, WOUT, WDIAG = 0, 12288, 12800, 20992
        self.wload(self.wmixe[e], 0, 20992, piece=4096)
        ident = self.cbf[:, 128:256]
        for ci in range(4):
            for j in range(31):
                c0 = WDIAG + (ci * 31 + j) * 128
                k.op("dve", lambda e_: e_.tensor_scalar_mul(self.W[:, c0:c0 + 128], ident, self.vcol("dw_w%d" % e, ci * 31 + j)),
                     reads=[self.cbf, self.vecs], writes=[self.slabs[c0 // SLAB]])
        xs = [k.sb("mx_xs%d" % i, [128, KC, TT], F32) for i in range(2)]
        h = k.sb("mx_h", [128, KC, TT], BF16)
        sq = k.sb("mx_sq", [128, KC, TT], BF16)
        U = [k.sb("mx_U%d" % g, [128, TT + 15], F32) for g in range(4)]
        AB = [k.sb("mx_AB%d" % i, [128, TT + 15], F32) for i in range(2)]
        pooled = [k.sb("mx_pl%d" % i, [128, TT], BF16) for i in range(2)]
        t16 = k.sb("mx_t16", [128, 16], F32)
        cat = k.sb("mx_cat", [128, KC, TT], BF16)
        sig = [k.sb("mx_sig%d" % i, [128, TT], F32) for i in range(2)]
        GL = [k.sb("mx_GL%d" % c, [128, TT + 30], BF16) for c in range(4)]
        CV = k.sb("mx_CV", [128, 4, TT], F32)
        XC = k.sb("mx_XC", [128, 4, TT], F32)
        SQ = k.sb("mx_SQ", [128, 4, TT], F32)
        rstd = k.sb("mx_rstd", [128, TT], F32)
        tmp = k.sb("mx_tmp", [128, TT], F32)
        lnr = k.sb("mx_lnr", [128, TT], F32)
        lnt = k.sb("mx_lnt", [128, TT], F32)
        yt = [k.sb("mx_yt%d" % i, [128, TT], F32) for i in range(2)]
        for g in range(4):
            k.op("dve", lambda e_: e_.memset(U[g][:, 0:15], 0.0), writes=[U[g]])
            k.op("dve", lambda e_: e_.memset(GL[g][:, 0:30], 0.0), writes=[GL[g]])
        self.load_x(xs[0], 0)
        for t in range(NT):
            x_ = xs[t % 2]
            if t + 1 < NT:
                self.load_x(xs[(t + 1) % 2], t + 1)
            self.rmsnorm(x_, KC, TT, "g_mix%d" % l, h, sq, rstd, tmp, D)

            def zmm(c):
                p = self.ps()
                for kc in range(KC):
                    wap, wsl = self.wv(WIN + kc * 1536 + c * 128, 128)
                    k.op("pe", lambda e_: e_.matmul(p[:], wap, h[:, kc, :], start=(kc == 0), stop=(kc == KC - 1)),
                         reads=[h] + wsl, writes=[p], sig=(kc == KC - 1))
                return p
            for g in range(4):
                w = 2 << g
                lvl = g + 1
                p = zmm(g)
                u = U[g]
                if t > 0:
                    k.op("dve", lambda e_: e_.tensor_copy(out=u[:, 0:15], in_=u[:, TT:TT + 15]), reads=[u], writes=[u])
                k.op("act", lambda e_: e_.activation(out=u[:, 15:TT + 15], in_=p[:], func=AF.Copy), reads=[p], writes=[u])
                need = [0] * (lvl + 1)
                need[lvl] = 15
                for kk in range(lvl, 0, -1):
                    need[kk - 1] = need[kk] - (1 << (kk - 1))
                src = u
                for kk in range(1, lvl + 1):
                    dst = AB[kk % 2]
                    sh = 1 << (kk - 1)
                    a0 = need[kk]
                    k.op("dve", lambda e_: e_.tensor_tensor(dst[:, a0:TT + 15], src[:, a0:TT + 15],
                                                            src[:, a0 - sh:TT + 15 - sh], ALU.add),
                         reads=[src], writes=[dst])
                    src = dst
                pl = pooled[g % 2]
                k.op("dve", lambda e_: e_.scalar_tensor_tensor(pl[:], src[:, 15:TT + 15], 1.0 / w, u[:, 15:TT + 15],
                                                               ALU.mult, ALU.subtract), reads=[src, u], writes=[pl])
                if t == 0:
                    o_ = VOFF["invcnt"] + g * 16
                    k.op("dve", lambda e_: e_.tensor_tensor(t16[:], src[:, 15:31], self.vecs[:, o_:o_ + 16], ALU.mult),
                         reads=[src, self.vecs], writes=[t16])
                    k.op("dve", lambda e_: e_.tensor_tensor(pl[:, 0:16], t16[:], u[:, 15:31], ALU.subtract),
                         reads=[t16, u], writes=[pl])
                py = self.ps()
                wap, wsl = self.wv(WPOOL + g * 128, 128)
                k.op("pe", lambda e_: e_.matmul(py[:], wap, pl[:], start=True, stop=True), reads=[pl] + wsl, writes=[py])
                k.op("act", lambda e_: e_.activation(out=cat[:, g, :], in_=py[:], func=AF.Copy,
                                                     scale=self.vcol("pool_scale%d" % e, g)),
                     reads=[py, self.vecs], writes=[cat])
            for ci in range(4):
                pa = zmm(4 + ci)
                pb = zmm(8 + ci)
                sg = sig[ci % 2]
                gl = GL[ci]
                k.op("act", lambda e_: e_.activation(out=sg[:], in_=pb[:], func=AF.Sigmoid), reads=[pb], writes=[sg])
                if t > 0:
                    k.op("dve", lambda e_: e_.tensor_copy(out=gl[:, 0:30], in_=gl[:, TT:TT + 30]), reads=[gl], writes=[gl])
                k.op("dve", lambda e_: e_.tensor_tensor(gl[:, 30:TT + 30], sg[:], pa[:], ALU.mult), reads=[sg, pa], writes=[gl])
                pc = self.ps()
                for j in range(31):
                    wap, wsl = self.wv(WDIAG + (ci * 31 + j) * 128, 128)
                    k.op("pe", lambda e_: e_.matmul(pc[:], wap, gl[:, j:j + TT], start=(j == 0), stop=(j == 30)),
                         reads=[gl] + wsl, writes=[pc], sig=(j == 30))
                k.op("act", lambda e_: e_.activation(out=CV[:, ci, :], in_=pc[:], func=AF.Identity,
                                                     bias=self.vcol("dw_b%d" % e, ci)), reads=[pc, self.vecs], writes=[CV])
            pm = self.ps()
            for ci in range(4):
                k.op("pe", lambda e_: e_.matmul(pm[:], self.ones32[:], CV[:, ci, :], start=(ci == 0), stop=(ci == 3)),
                     reads=[CV, self.ones32], writes=[pm], sig=(ci == 3))
            for ci in range(4):
                k.op("dve", lambda e_: e_.scalar_tensor_tensor(XC[:, ci, :], pm[:], -1.0 / 512.0, CV[:, ci, :],
                                                               ALU.mult, ALU.add), reads=[pm, CV], writes=[XC])
            k.op("act", lambda e_: e_.activation(out=SQ[:], in_=XC[:], func=AF.Square), reads=[XC], writes=[SQ])
            pv = self.ps()
            for ci in range(4):
                k.op("pe", lambda e_: e_.matmul(pv[:], self.ones32[:], SQ[:, ci, :], start=(ci == 0), stop=(ci == 3)),
                     reads=[SQ, self.ones32], writes=[pv], sig=(ci == 3))
            k.op("act", lambda e_: e_.activation(out=lnt[:], in_=pv[:], func=AF.Sqrt, scale=1.0 / 512.0,
                                                 bias=self.epsb[:, 0:1]), reads=[pv, self.epsb], writes=[lnt])
            k.op("dve", lambda e_: e_.reciprocal(lnr[:], lnt[:]), reads=[lnt], writes=[lnr])
            for ci in range(4):
                y_ = yt[ci % 2]
                k.op("dve", lambda e_: e_.scalar_tensor_tensor(y_[:], XC[:, ci, :], self.vcol("ln_g%d" % e, ci), lnr[:],
                                                               ALU.mult, ALU.mult), reads=[XC, lnr, self.vecs], writes=[y_])
                k.op("act", lambda e_: e_.activation(out=cat[:, 4 + ci, :], in_=y_[:], func=AF.Silu,
                                                     bias=self.vcol("ln_b%d" % e, ci)), reads=[y_, self.vecs], writes=[cat])
            for c in range(KC):
                p = self.ps()
                for kc in range(KC):
                    wap, wsl = self.wv(WOUT + kc * 1024 + c * 128, 128)
                    k.op("pe", lambda e_: e_.matmul(p[:], wap, cat[:, kc, :], start=(kc == 0), stop=(kc == KC - 1)),
                         reads=[cat] + wsl, writes=[p], sig=(kc == KC - 1))
                k.op("dve", lambda e_: e_.tensor_tensor(x_[:, c, :], x_[:, c, :], p[:], ALU.add), reads=[x_, p], writes=[x_])
            self.store_x(x_, t)
        self.first_x = False
        k.end_phase()

    def ph_rope(self):
        k = self.k
        k.begin_phase()
        import math
        MAGIC = 8388608.0
        C1 = 6.28125
        C2 = 2.0 * math.pi - 6.28125
        PIS = 3.1415925
        nb = lambda n: [k.sb("rp_%s%d" % (n, i), [128, TT], F32) for i in range(2)]
        posi = [k.sb("rp_posi%d" % i, [128, TT], I32) for i in range(2)]
        posf, ang, kf, kr, r1, r2, r3, sn, ss, ab, cs = (nb(n) for n in
                                                         ("posf", "ang", "kf", "kr", "r1", "r2", "r3", "sn", "ss", "ab", "cs"))
        hpi = k.sb("rp_hpi", [128, 1], F32)
        k.op("dve", lambda e: e.memset(hpi[:], math.pi / 2.0), writes=[hpi])
        for t in range(self.NT):
            i = t % 2
            k.dma("sp", posi[i][:], self.pos[:, t * TT:(t + 1) * TT], writes=[posi[i]],
                  sembuf=posi[i])
            k.op("dve", lambda e: e.tensor_copy(out=posf[i][:], in_=posi[i][:]), reads=[posi[i]], writes=[posf[i]])
            k.op("dve", lambda e: e.tensor_scalar_mul(ang[i][:], posf[i][:], self.vcol("inv128")),
                 reads=[posf[i], self.vecs], writes=[ang[i]])
            k.op("dve", lambda e: e.tensor_scalar(kf[i][:], ang[i][:], 1.0 / (2.0 * math.pi), MAGIC, ALU.mult, ALU.add),
                 reads=[ang[i]], writes=[kf[i]])
            k.op("dve", lambda e: e.tensor_scalar_add(kr[i][:], kf[i][:], -MAGIC), reads=[kf[i]], writes=[kr[i]])
            k.op("dve", lambda e: e.scalar_tensor_tensor(r1[i][:], kr[i][:], -C1, ang[i][:], ALU.mult, ALU.add),
                 reads=[kr[i], ang[i]], writes=[r1[i]])
            k.op("dve", lambda e: e.scalar_tensor_tensor(r2[i][:], kr[i][:], -C2, r1[i][:], ALU.mult, ALU.add),
                 reads=[kr[i], r1[i]], writes=[r2[i]])
            k.op("dve", lambda e: e.tensor_scalar(r3[i][:], r2[i][:], -PIS, PIS, ALU.max, ALU.min),
                 reads=[r2[i]], writes=[r3[i]])
            k.op("act", lambda e: e.activation(out=sn[i][:], in_=r3[i][:], func=AF.Sin), reads=[r3[i]], writes=[sn[i]])
            k.op("dve", lambda e: e.tensor_scalar_mul(ss[i][:], sn[i][:], self.vcol("sgn128")),
                 reads=[sn[i], self.vecs], writes=[ss[i]])
            k.op("act", lambda e: e.activation(out=ab[i][:], in_=r3[i][:], func=AF.Abs), reads=[r3[i]], writes=[ab[i]])
            k.op("act", lambda e: e.activation(out=cs[i][:], in_=ab[i][:], func=AF.Sin, scale=-1.0, bias=hpi[:, 0:1]),
                 reads=[ab[i], hpi], writes=[cs[i]])
            k.dma("sp", self.ropeC[:, t * TT:(t + 1) * TT], cs[i][:], reads=[cs[i]], writes=[self.d_rope[t]], sembuf=cs[i])
            k.dma("sp", self.ropeS[:, t * TT:(t + 1) * TT], ss[i][:], reads=[ss[i]], writes=[self.d_rope[t]], sembuf=ss[i])
        k.end_phase()

    def ph_mla1(self, l):
        k = self.k
        NT = self.NT
        o = l // 2
        k.begin_phase()
        WA, WB, WC = 0, 6144, 13824
        self.wload(self.wmla[o][:, 0:18944], 0, 18944, piece=4096)
        SCALE = 1.0 / (96.0 ** 0.5)
        xs = [k.sb("m1_xs%d" % i, [128, KC, TT], F32) for i in range(2)]
        rc = [k.sb("m1_rc%d" % i, [128, TT], F32) for i in range(2)]
        rs = [k.sb("m1_rs%d" % i, [128, TT], F32) for i in range(2)]
        h = k.sb("m1_h", [128, KC, TT], BF16)
        sq = k.sb("m1_sq", [128, KC, TT], BF16)
        rstd = k.sb("m1_rstd", [128, TT], F32)
        tmp = k.sb("m1_tmp", [128, TT], F32)
        CQ = k.sb("m1_CQ", [128, 3, TT], F32)
        CKV = k.sb("m1_CKV", [128, 2, TT], F32)
        cqn = k.sb("m1_cqn", [128, 3, TT], BF16)
        ckvn = k.sb("m1_ckvn", [128, 2, TT], BF16)
        t1 = [k.sb("m1_t1%d" % i, [128, TT], F32) for i in range(2)]
        t2 = [k.sb("m1_t2%d" % i, [128, TT], F32) for i in range(2)]
        KR = k.sb("m1_KR", [128, TT], BF16)
        QPE = k.sb("m1_QPE", [128, 4, TT], BF16)
        QS = k.sb("m1_QS", [128, 16, TT], BF16)
        KS = k.sb("m1_KS", [128, 16, TT], BF16)
        VS = k.sb("m1_VS", [128, 4, 16, 65], BF16)
        k.op("dve", lambda e: e.memset(VS[:], 1.0), writes=[VS])

        def load(t):
            i = t % 2
            self.load_x(xs[i], t)
            k.dma("sp", rc[i][:], self.ropeC[:, t * TT:(t + 1) * TT], reads=[self.d_rope[t]], writes=[rc[i]], sembuf=rc[i])
            k.dma("sp", rs[i][:], self.ropeS[:, t * TT:(t + 1) * TT], reads=[self.d_rope[t]], writes=[rs[i]], sembuf=rs[i])

        def mm(p, M, c0, nkc, stride, rhs_buf, extra=None):
            for kc in range(nkc):
                wap, wsl = self.wv(c0 + kc * stride, M)
                last = (kc == nkc - 1) and extra is None
                k.op("pe", lambda e: e.matmul(p[0:M, :], wap, rhs_buf[:, kc, :], start=(kc == 0), stop=last),
                     reads=[rhs_buf] + wsl, writes=[p], sig=last)
            if extra is not None:
                lhsT, rhs, rb = extra
                k.op("pe", lambda e: e.matmul(p[0:M, :], lhsT, rhs, start=False, stop=True),
                     reads=[rb, self.cbf], writes=[p])

        def rope(pa, pb, np_, i, out_ap, out_buf, ti):
            a, b = t1[ti], t2[ti]
            k.op("dve", lambda e: e.tensor_tensor(a[0:np_, :], pa[0:np_, :], rc[i][0:np_, :], ALU.mult),
                 reads=[pa, rc[i]], writes=[a])
            k.op("dve", lambda e: e.tensor_tensor(b[0:np_, :], pb[0:np_, :], rs[i][0:np_, :], ALU.mult),
                 reads=[pb, rs[i]], writes=[b])
            k.op("dve", lambda e: e.tensor_tensor(out_ap, a[0:np_, :], b[0:np_, :], ALU.add), reads=[a, b], writes=[out_buf])

        load(0)
        for t in range(NT):
            i = t % 2
            x_ = xs[i]
            if t + 1 < NT:
                load(t + 1)
            self.rmsnorm(x_, KC, TT, "g_mix%d" % l, h, sq, rstd, tmp, D)
            for c in range(3):
                p = self.ps()
                mm(p, 128, WA + c * 128, KC, 768, h)
                k.op("act", lambda e: e.activation(out=CQ[:, c, :], in_=p[:], func=AF.Copy), reads=[p], writes=[CQ])
            for c in range(2):
                p = self.ps()
                mm(p, 128, WA + 384 + c * 128, KC, 768, h)
                k.op("dve", lambda e: e.tensor_copy(out=CKV[:, c, :], in_=p[:]), reads=[p], writes=[CKV])
            if getattr(self, "debug", False):
                k.dma("sp", self.xtile_ap(self.oT, t), h[:], reads=[h], writes=[self.d_o[t]], sembuf=h)
            p1 = self.ps()
            mm(p1, 32, WA + 640, KC, 768, h)
            p2 = self.ps()
            mm(p2, 32, WA + 672, KC, 768, h)
            rope(p1, p2, 32, i, KR[0:32, :], KR, 0)
            self.rmsnorm(CQ, 3, TT, "g_q%d" % o, cqn, sq, rstd, tmp, 384)
            self.rmsnorm(CKV, 2, TT, "g_kv%d" % o, ckvn, sq, rstd, tmp, 256)
            for c in range(4):
                pa = self.ps()
                mm(pa, 128, WB + c * 128, 3, 2560, cqn)
                pb = self.ps()
                mm(pb, 128, WB + 512 + c * 128, 3, 2560, cqn)
                rope(pa, pb, 128, i, QPE[:, c, :], QPE, c % 2)
            for hh in range(16):
                p = self.ps()
                sel = self.cbf[:, 384 + (hh % 4) * 96:384 + (hh % 4) * 96 + 96]
                mm(p, 96, WB + 1024 + hh * 96, 3, 2560, cqn, extra=(sel, QPE[:, hh // 4, :], QPE))
                if hh % 2 == 0:
                    k.op("act", lambda e: e.activation(out=QS[0:96, hh, :], in_=p[0:96, :], func=AF.Copy, scale=SCALE),
                         reads=[p], writes=[QS])
                else:
                    k.op("dve", lambda e: e.tensor_scalar_mul(QS[0:96, hh, :], p[0:96, :], SCALE), reads=[p], writes=[QS])
            for hh in range(16):
                p = self.ps()
                selk = self.cbf[0:32, 768:864]
                mm(p, 96, WC + hh * 96, 2, 2560, ckvn, extra=(selk, KR[0:32, :], KR))
                if hh % 2 == 1:
                    k.op("act", lambda e: e.activation(out=KS[0:96, hh, :], in_=p[0:96, :], func=AF.Copy), reads=[p], writes=[KS])
                else:
                    k.op("dve", lambda e: e.tensor_copy(out=KS[0:96, hh, :], in_=p[0:96, :]), reads=[p], writes=[KS])
            for ts in range(4):
                for hf in range(2):
                    p = self.ps()
                    for kc in range(2):
                        wap, wsl = self.wv(WC + kc * 2560 + 1536 + hf * 512, 512)
                        k.op("pe", lambda e: e.matmul(p[:], ckvn[:, kc, ts * 128:(ts + 1) * 128], wap, start=(kc == 0),
                                                      stop=(kc == 1)), reads=[ckvn] + wsl, writes=[p], sig=(kc == 1))
                    src = p[:].rearrange("p (h d) -> p h d", d=64)
                    if (ts + hf) % 2 == 0:
                        k.op("act", lambda e: e.activation(out=VS[:, ts, hf * 8:(hf + 1) * 8, 0:64], in_=src, func=AF.Copy),
                             reads=[p], writes=[VS])
                    else:
                        k.op("dve", lambda e: e.tensor_copy(out=VS[:, ts, hf * 8:(hf + 1) * 8, 0:64], in_=src),
                             reads=[p], writes=[VS])
            k.dma("sp", self.qT.rearrange("h p t -> p h t")[:, :, t * TT:(t + 1) * TT], QS[0:96, :, :], reads=[QS],
                  writes=[self.d_q[t]], sembuf=QS)
            k.dma("sp", self.kT.rearrange("h p t -> p h t")[:, :, t * TT:(t + 1) * TT], KS[0:96, :, :], reads=[KS],
                  writes=[self.d_k[t]], sembuf=KS)
            k.dma("sp", self.vS.rearrange("(n s p) (h d) -> n p s h d", s=4, p=128, d=65)[t], VS[:], reads=[VS],
                  writes=[self.d_v[t]], sembuf=VS)
        k.end_phase()

    def ph_mla2(self, l):
        k = self.k
        NT = self.NT
        T = self.T
        NKC = T // 128
        k.begin_phase()
        KT = [k.sb("m2_KT%d" % i, [128, T], BF16) for i in range(2)]
        QT = [k.sb("m2_QT%d" % i, [128, T], BF16) for i in range(2)]
        VV = [k.sb("m2_VV%d" % i, [128, NKC, 65], BF16) for i in range(2)]
        rden = [k.sb("m2_rden%d" % i, [128, TT], F32) for i in range(2)]
        bcs = [k.sb("m2_bcs%d" % i, [128, TT], F32) for i in range(2)]
        ONt = [k.sb("m2_ON%d" % i, [128, TT], BF16) for i in range(2)]
        tri = self.cbf[:, 256:384]
        allq = list(self.d_q)
        allk = list(self.d_k)
        allv = list(self.d_v)

        def loadh(hh):
            i = hh % 2
            k.dma("sp", KT[i][0:96, :], self.kT[hh], reads=allk, writes=[KT[i]], sembuf=KT[i])
            k.dma("sp", QT[i][0:96, :], self.qT[hh], reads=allq, writes=[QT[i]], sembuf=QT[i])
            k.dma("sp", VV[i][:], self.vS.rearrange("(c p) f -> p c f", p=128)[:, :, hh * 65:(hh + 1) * 65],
                  reads=allv, writes=[VV[i]], sembuf=VV[i])

        PT = [k.sb("m2_PTr%d" % i, [128, 3, TT], BF16) for i in range(4)]
        groups = []
        gi = 0
        ti = 0
        for hh in range(16):
            for j in range(NT):
                nd = list(range(4 * j))
                glist = []
                pos_ = 0
                while pos_ < len(nd):
                    n = 3 if gi % 2 == 0 else 2
                    glist.append((gi, nd[pos_:pos_ + n], 0))
                    pos_ += n
                    gi += 1
                for d in range(4):
                    glist.append((gi, [4 * j + d], 128 * d))
                    gi += 1
                for idx, (g, kcs, c_lo) in enumerate(glist):
                    groups.append(dict(g=g, kcs=kcs, c_lo=c_lo, hh=hh, j=j, ti=ti, first=(idx == 0),
                                       last=(idx == len(glist) - 1), diag=(kcs[0] >= 4 * j),
                                       first_of_head=(j == 0 and idx == 0)))
                ti += 1

        def stage1(G):
            g, kcs, c_lo, hh, j = G["g"], G["kcs"], G["c_lo"], G["hh"], G["j"]
            kt, qt = KT[hh % 2], QT[hh % 2]
            b0 = 0 if g % 2 == 0 else 3
            n = len(kcs)
            banks = [self.PS[b0 + ii] for ii in range(n)]
            pt = PT[g % 4]
            for ii, kc in enumerate(kcs):
                k.op("pe", lambda e: e.matmul(banks[ii][:, c_lo:TT], kt[0:96, kc * 128:(kc + 1) * 128],
                                              qt[0:96, j * TT + c_lo:(j + 1) * TT], start=True, stop=True),
                     reads=[kt, qt], writes=[banks[ii]], sig=(ii == n - 1))
            if not G["diag"]:
                k.op("act", lambda e: e.activation(out=pt[:, 0:n, :], in_=self.pst[:, b0 * 512:(b0 + n) * 512]
                                                   .rearrange("p (n t) -> p n t", t=TT), func=AF.Exp),
                     reads=banks, writes=[pt])
            else:
                k.op("act", lambda e: e.activation(out=pt[:, 0, c_lo:TT], in_=banks[0][:, c_lo:TT], func=AF.Exp),
                     reads=banks, writes=[pt])
                k.op("dve", lambda e: e.tensor_tensor(pt[:, 0, c_lo:c_lo + 128], pt[:, 0, c_lo:c_lo + 128], tri, ALU.mult),
                     reads=[pt, self.cbf], writes=[pt])

        def stage2(G):
            g, kcs, c_lo, hh = G["g"], G["kcs"], G["c_lo"], G["hh"]
            vv = VV[hh % 2]
            Ob = self.PS[OBANK[G["ti"] % 2]]
            pt = PT[g % 4]
            n = len(kcs)
            for ii, kc in enumerate(kcs):
                k.op("pe", lambda e: e.matmul(Ob[0:65, c_lo:TT], vv[:, kc, 0:65], pt[:, ii, c_lo:TT],
                                              start=(G["first"] and ii == 0), stop=(G["last"] and ii == n - 1)),
                     reads=[vv, pt], writes=[Ob], sig=(ii == n - 1))

        def fin_a(G):
            f = G["ti"] % 2
            Ob = self.PS[OBANK[f]]
            k.op("dve", lambda e: e.reciprocal(rden[f][64:65, :], Ob[64:65, :]), reads=[Ob], writes=[rden[f]])

        def fin_b(G):
            f = G["ti"] % 2
            hh, j = G["hh"], G["j"]
            Ob = self.PS[OBANK[f]]
            rd, bc, on = rden[f], bcs[f], ONt[f]
            pb = self.PS[BCBANK]
            k.op("pe", lambda e: e.matmul(pb[0:64, :], self.ones32[64:65, 0:64], rd[64:65, :], start=True, stop=True),
                 reads=[rd, self.ones32], writes=[pb])
            k.op("act", lambda e: e.activation(out=bc[0:64, :], in_=pb[0:64, :], func=AF.Copy), reads=[pb], writes=[bc])
            k.op("dve", lambda e: e.tensor_tensor(on[0:64, :], Ob[0:64, :], bc[0:64, :], ALU.mult),
                 reads=[Ob, bc], writes=[on])
            k.dma("sp", self.oT[hh * 64:(hh + 1) * 64, j * TT:(j + 1) * TT], on[0:64, :], reads=[on],
                  writes=[self.d_o[j]], sembuf=on)

        loadh(0)
        prev = None
        deferred = None
        for G in groups:
            stage1(G)
            if deferred is not None:
                fin_b(deferred)
                deferred = None
            if prev is not None:
                stage2(prev)
                if prev["last"]:
                    fin_a(prev)
                    if NODEFER:
                        fin_b(prev)
                    else:
                        deferred = prev
            if G["first_of_head"] and G["hh"] + 1 < 16:
                loadh(G["hh"] + 1)
            prev = G
        if deferred is not None:
            fin_b(deferred)
        stage2(prev)
        fin_a(prev)
        fin_b(prev)
        k.end_phase()

    def ph_mla3(self, l):
        k = self.k
        NT = self.NT
        o = l // 2
        k.begin_phase()
        WD = 18944
        self.wload(self.wmla[o][:, 18944:27136], WD, 8192, piece=4096)
        xs = [k.sb("m3_xs%d" % i, [128, KC, TT], F32) for i in range(2)]
        ot = [k.sb("m3_ot%d" % i, [128, KC, TT], BF16) for i in range(2)]

        def load(t):
            self.load_x(xs[t % 2], t)
            k.dma("sp", ot[t % 2][:], self.xtile_ap(self.oT, t), reads=[self.d_o[t]], writes=[ot[t % 2]], sembuf=ot[t % 2])
        load(0)
        for t in range(NT):
            x_ = xs[t % 2]
            o_ = ot[t % 2]
            if t + 1 < NT:
                load(t + 1)
            for c in range(KC):
                p = self.ps()
                for kc in range(KC):
                    wap, wsl = self.wv(WD + kc * 1024 + c * 128, 128)
                    k.op("pe", lambda e: e.matmul(p[:], wap, o_[:, kc, :], start=(kc == 0), stop=(kc == KC - 1)),
                         reads=[o_] + wsl, writes=[p], sig=(kc == KC - 1))
                k.op("dve", lambda e: e.tensor_tensor(x_[:, c, :], x_[:, c, :], p[:], ALU.add), reads=[x_, p], writes=[x_])
            self.store_x(x_, t)
        self.first_x = False
        k.end_phase()

    def ph_final(self):
        k = self.k
        NT = self.NT
        k.begin_phase()
        xs = [k.sb("fn_xs%d" % i, [128, KC, TT], F32) for i in range(2)]
        ys = [k.sb("fn_ys%d" % i, [128, KC, TT], F32) for i in range(2)]
        sq = k.sb("fn_sq", [128, KC, TT], BF16)
        rstd = k.sb("fn_rstd", [128, TT], F32)
        tmp = k.sb("fn_tmp", [128, TT], F32)
        self.load_x(xs[0], 0)
        for t in range(NT):
            if t + 1 < NT:
                self.load_x(xs[(t + 1) % 2], t + 1)
            y_ = ys[t % 2]
            self.rmsnorm(xs[t % 2], KC, TT, "g_final", y_, sq, rstd, tmp, D)
            k.dma("sp", self.xtile_ap(self.yout, t), y_[:], reads=[y_], writes=[self.d_y], sembuf=y_)
        k.end_phase()

    def ph_dump(self):
        k = self.k
        T = self.T
        k.begin_phase()
        allr = self.d_rope + self.d_q + self.d_k + self.d_v + self.d_o
        a = k.sb("dbg_a", [128, T], BF16)
        b = k.sb("dbg_b", [128, T], F32)
        k.dma("sp", self.yout[0:128, :], self.ropeC[:, :], reads=allr, writes=[self.d_y], sembuf=a)
        k.dma("sp", self.yout[128:256, :], self.ropeS[:, :], reads=allr, writes=[self.d_y], sembuf=a)
        def cp(src, n, r0):
            k.dma("sp", a[0:n, :], src, reads=allr, writes=[a], sembuf=a)
            k.op("dve", lambda e: e.tensor_copy(out=b[0:n, :], in_=a[0:n, :]), reads=[a], writes=[b])
            k.dma("sp", self.yout[r0:r0 + n, :], b[0:n, :], reads=[b], writes=[self.d_y], sembuf=b)
        cp(self.qT[0], 96, 256)
        cp(self.kT[0], 96, 352)
        cp(self.vS[0:128, 0:1024], 128, 448) if T <= 1024 else None
        cp(self.oT[0:128, :], 128, 576)
        cp(self.oT[128:256, :], 128, 704)
        cp(self.qT[5], 96, 832)
        cp(self.kT[5], 96, 928)
        k.end_phase()


def _fm(v):
    v = np.asarray(v, np.float32)
    return np.ascontiguousarray(v.reshape(-1, 128).T)


def prep_shared(inp):
    f32 = np.float32
    vecs = np.zeros((128, NV), f32)

    def put(name, arr):
        arr = np.asarray(arr, f32)
        vecs[:, VOFF[name]:VOFF[name] + arr.shape[1]] = arr

    for l in range(4):
        put("g_mix%d" % l, _fm(inp["norm_mix_g"][l]))
        put("g_xa%d" % l, _fm(inp["norm_xa_g"][l]))
        put("g_mem%d" % l, _fm(inp["norm_mem_g"][l]))
        put("g_ffn%d" % l, _fm(inp["norm_ffn_g"][l]))
        cw = np.asarray(inp["ffn_conv_w"][l], f32)
        cb = np.asarray(inp["ffn_conv_b"][l], f32)
        a = np.stack([cw[0], cw[1], cw[2], cb], axis=-1)
        a = a.reshape(NJ, 128, 4).transpose(1, 0, 2).reshape(128, NJ * 4)
        put("ffn_cw%d" % l, a)
    for e in range(2):
        put("pool_scale%d" % e, _fm(inp["pool_scale"][e]))
        dw = np.asarray(inp["conv_dw_w"][e], f32)
        a = dw.T.reshape(4, 128, 31).transpose(1, 0, 2).reshape(128, 4 * 31)
        put("dw_w%d" % e, a)
        put("dw_b%d" % e, _fm(inp["conv_dw_b"][e]))
        put("ln_g%d" % e, _fm(inp["conv_ln_g"][e]))
        put("ln_b%d" % e, _fm(inp["conv_ln_b"][e]))
        put("g_q%d" % e, _fm(inp["mla_q_norm_g"][e]))
        put("g_kv%d" % e, _fm(inp["mla_kv_norm_g"][e]))
    put("g_final", _fm(inp["final_norm_g"]))
    inv = (1.0 / (np.float32(10000.0) ** (np.arange(0, 32, 2, dtype=f32) / np.float32(32)))).astype(f32)
    p = np.arange(128)
    put("inv128", inv[p % 16][:, None])
    put("sgn128", np.where((p % 32) < 16, -1.0, 1.0).astype(f32)[:, None])
    ic = np.zeros((128, 64), f32)
    for gi, w in enumerate((2, 4, 8, 16)):
        ic[:, gi * 16:(gi + 1) * 16] = 1.0 / np.minimum(np.arange(16) + 1, w).astype(f32)[None, :]
    put("invcnt", ic)

    cbf = np.zeros((128, 1024), f32)
    cbf[:, 0:128] = 1.0
    cbf[:, 128:256] = np.eye(128, dtype=f32)
    cbf[:, 256:384] = (p[:, None] <= p[None, :]).astype(f32)
    for a_ in range(4):
        for i in range(32):
            cbf[32 * a_ + i, 384 + a_ * 96 + i] = 1.0
    for i in range(32):
        cbf[i, 768 + i] = 1.0

    wffn = np.zeros((4, 2, 128, NJH * 3072), f32)
    for l in range(4):
        wu = np.asarray(inp["ffn_w_up"][l], f32).reshape(KC, 128, 2, NJ, 128)
        wd = np.asarray(inp["ffn_w_down"][l], f32).reshape(NJ, 128, D)
        up = wu.transpose(3, 1, 0, 2, 4).reshape(NJ, 128, KC * 256)
        for s in range(2):
            blk = np.concatenate([up[s * NJH:(s + 1) * NJH], wd[s * NJH:(s + 1) * NJH]], axis=2)
            wffn[l, s] = blk.transpose(1, 0, 2).reshape(128, NJH * 3072)
    sh = {"vecs": vecs, "cbf": cbf, "wffn": wffn}
    sh.update(prep_shared2(inp))
    return sh


def _prep_mixe(inp):
    f32 = np.float32
    out = np.zeros((2, 128, 20992), f32)
    for e in range(2):
        win = np.asarray(inp["pc_w_in"][e], f32).reshape(KC, 128, 1536).transpose(1, 0, 2).reshape(128, KC * 1536)
        pw = np.asarray(inp["pool_w"][e], f32).transpose(1, 0, 2).reshape(128, 512)
        wo = np.asarray(inp["pc_w_out"][e], f32).reshape(KC, 128, D).transpose(1, 0, 2).reshape(128, KC * D)
        out[e] = np.concatenate([win, pw, wo], axis=1)
    return out


def _prep_mla(inp):
    f32 = np.float32
    out = np.zeros((2, 128, 27136), f32)
    swp = (np.arange(32) + 16) % 32
    for o in range(2):
        wd = np.asarray(inp["mla_w_dq_dkv"][o], f32)
        A = np.zeros((D, 768), f32)
        A[:, 0:672] = wd
        A[:, 672:704] = wd[:, 640:672][:, swp]
        wq = np.asarray(inp["mla_w_uq"][o], f32).reshape(384, 16, 96)
        B = np.zeros((384, 2560), f32)
        pe = wq[:, :, 64:96]
        B[:, 0:512] = pe.reshape(384, 512)
        B[:, 512:1024] = pe[:, :, swp].reshape(384, 512)
        Bh = np.zeros((384, 16, 96), f32)
        Bh[:, :, 32:96] = wq[:, :, 0:64]
        B[:, 1024:2560] = Bh.reshape(384, 1536)
        wk = np.asarray(inp["mla_w_ukv"][o], f32).reshape(256, 16, 128)
        C = np.zeros((256, 2560), f32)
        Ch = np.zeros((256, 16, 96), f32)
        Ch[:, :, 32:96] = wk[:, :, 0:64]
        C[:, 0:1536] = Ch.reshape(256, 1536)
        C[:, 1536:2560] = wk[:, :, 64:128].reshape(256, 1024)
        wo = np.asarray(inp["mla_w_o"][o], f32)
        fm = lambda w: w.reshape(-1, 128, w.shape[1]).transpose(1, 0, 2).reshape(128, -1)
        out[o] = np.concatenate([fm(A), fm(B), fm(C), fm(wo)], axis=1)
    return out


def prep_shared2(inp):
    f32 = np.float32
    wxa = np.zeros((4, 128, 16384), f32)
    wkv = np.zeros((4, 128, 16384), f32)
    for l in range(4):
        wq = np.asarray(inp["xa_wq"][l], f32).reshape(KC, 128, D).transpose(1, 0, 2).reshape(128, KC * D)
        wo = np.asarray(inp["xa_wo"][l], f32).reshape(KC, 128, D).transpose(1, 0, 2).reshape(128, KC * D)
        wxa[l] = np.concatenate([wq, wo], axis=1)
        wkv[l] = np.asarray(inp["xa_wkv"][l], f32).reshape(KC, 128, 2 * D).transpose(1, 0, 2).reshape(128, KC * 2 * D)
    return {"wxa": wxa, "wkv": wkv,
            "wmixe": _prep_mixe(inp), "wmla": _prep_mla(inp)}


ALL_PHASES = []


def run_prog(prog, shared, xT_list, memT_list, pos_list):
    in_maps = []
    for c in range(len(xT_list)):
        m = dict(shared)
        m["xT"] = xT_list[c]
        m["memT"] = memT_list[c]
        m["pos"] = pos_list[c]
        in_maps.append(m)
    res = run_bass_kernel_spmd(prog.nc, in_maps, core_ids=list(range(len(xT_list))))
    return [r["yT"] for r in res.results]


def all_phases():
    ph = [("rope",)] + [("memkv", l) for l in range(4)]
    for l in range(4):
        if l % 2 == 0:
            ph.append(("mixe", l))
        else:
            ph += [("mla1", l), ("mla2", l), ("mla3", l)]
        ph += [("xa", l), ("ffn", l, 0), ("ffn", l, 1)]
    ph.append(("final",))
    return ph


def kernel(**inputs):
    x = np.asarray(inputs["x"], np.float32)
    mem = np.asarray(inputs["mem"], np.float32)
    positions = np.asarray(inputs["positions"]).astype(np.int32)
    B, T, _ = x.shape
    NT = T // TT
    shared = prep_shared(inputs)
    prog = Prog(NT, all_phases())
    xT = [np.ascontiguousarray(x[b].T) for b in range(B)]
    memT = [np.ascontiguousarray(mem[b].T) for b in range(B)]
    pos = [np.ascontiguousarray(np.broadcast_to(positions[b][None, :], (128, T))) for b in range(B)]
    ys = run_prog(prog, shared, xT, memT, pos)
    return np.stack([np.ascontiguousarray(y.T) for y in ys]).astype(np.float32)
```

```python
import contextlib
import numpy as np
import concourse.bass as bass
import concourse.mybir as mybir
from concourse.bass_utils import run_bass_kernel_spmd

F32 = mybir.dt.float32
BF16 = mybir.dt.bfloat16
I32 = mybir.dt.int32
AF = mybir.ActivationFunctionType
ALU = mybir.AluOpType

D = 1024
KC = 8
TT = 512
DFF = 2816
NJ = 22
NJH = 11
EPS = 1e-6
MEM = 256
SLAB = 512
NSLAB = 72
SELF_SYNC = True
NDSEM = 56
NODEFER = False
OBANK = (5, 7)
BCBANK = 6
SELF_DIST = 2

VOFF = {}
_nv = 0


def _valloc(name, n):
    global _nv
    VOFF[name] = _nv
    _nv += n


for _l in range(4):
    _valloc("g_mix%d" % _l, 8)
    _valloc("g_xa%d" % _l, 8)
    _valloc("g_mem%d" % _l, 8)
    _valloc("g_ffn%d" % _l, 8)
    _valloc("ffn_cw%d" % _l, NJ * 4)
for _e in range(2):
    _valloc("pool_scale%d" % _e, 4)
    _valloc("dw_w%d" % _e, 4 * 31)
    _valloc("dw_b%d" % _e, 4)
    _valloc("ln_g%d" % _e, 4)
    _valloc("ln_b%d" % _e, 4)
    _valloc("g_q%d" % _e, 3)
    _valloc("g_kv%d" % _e, 2)
_valloc("g_final", 8)
_valloc("inv128", 1)
_valloc("sgn128", 1)
_valloc("invcnt", 4 * 16)
NV = _nv


class Buf:
    __slots__ = ("name", "t", "last_w", "reads", "dsem", "dcnt")

    def __init__(self, name, t=None):
        self.name = name
        self.t = t
        self.last_w = None
        self.reads = {}
        self.dsem = None
        self.dcnt = 0

    def __getitem__(self, idx):
        return self.t[idx]


class PBuf(Buf):
    __slots__ = ("off",)

    def __init__(self, name, t, off):
        Buf.__init__(self, name, t)
        self.off = off

    def __getitem__(self, idx):
        if not isinstance(idx, tuple):
            idx = (idx, slice(None))
        ps, cs = idx
        a = cs.start or 0
        b = 512 if cs.stop is None else cs.stop
        return self.t[ps, self.off + a:self.off + b]


class KB:
    def __init__(self, nc):
        self.nc = nc
        self.E = {"pe": nc.tensor, "act": nc.scalar, "dve": nc.vector, "pool": nc.gpsimd, "sp": nc.sync}
        self.semh = {k: nc.alloc_semaphore("e_" + k) for k in self.E}
        self.cnt = {k: 0 for k in self.E}
        self.known = {k: {} for k in self.E}
        self.dsems = {}
        self.free_dsems = []
        self.ndsem = 0
        for _ in range(NDSEM):
            key = "d%d" % self.ndsem
            self.ndsem += 1
            self.semh[key] = nc.alloc_semaphore(key)
            self.dsems[key] = 0
            self.free_dsems.append(key)
        for key in self.semh:
            nc.gpsimd.sem_clear(self.semh[key])
        nc.all_engine_barrier()
        self.stack = None
        self.local = []
        self.phase_no = 0

    def sb(self, name, shape, dt, persistent=False):
        if persistent or self.stack is None:
            b = Buf(name, self.nc.alloc_sbuf_tensor(name, shape, dt))
        else:
            name = "%s_p%d" % (name, self.phase_no)
            t = self.stack.enter_context(self.nc.sbuf_tensor(name, shape, dt))
            b = Buf(name, t)
            self.local.append(b)
        return b

    def dram(self, name):
        return Buf(name, None)

    def _dsem(self, b):
        if b.dsem is None:
            if self.free_dsems:
                key = self.free_dsems.pop()
            else:
                key = "d%d" % self.ndsem
                self.ndsem += 1
                self.semh[key] = self.nc.alloc_semaphore(key)
                self.dsems[key] = 0
            b.dsem = key
            b.dcnt = self.dsems[key]
        return b.dsem

    def _wait(self, eng, deps):
        kn = self.known[eng]
        for (s, v) in deps:
            if s == eng and not (SELF_SYNC and eng in ("dve", "act") and v > self.cnt[eng] - SELF_DIST):
                continue
            if kn.get(s, 0) >= v:
                continue
            self.E[eng].wait_ge(self.semh[s], v)
            kn[s] = v

    @staticmethod
    def _deps(reads, writes, skip_sem=None):
        deps = []
        for b in reads:
            if b.last_w is not None:
                deps.append(b.last_w)
        for b in writes:
            if b.last_w is not None and b.last_w[0] != skip_sem:
                deps.append(b.last_w)
            deps.extend(b.reads.items())
        return deps

    def op(self, eng, fn, reads=(), writes=(), sig=True):
        self._wait(eng, self._deps(reads, writes))
        ins = fn(self.E[eng])
        if sig:
            self.cnt[eng] += 1
            ins.then_inc(self.semh[eng], 1)
            v = self.cnt[eng]
        else:
            v = self.cnt[eng] + 1
        for b in reads:
            if b.reads.get(eng, 0) < v:
                b.reads[eng] = v
        for b in writes:
            b.last_w = (eng, v)
            b.reads = {}
        return ins

    def dma(self, q, out, in_, reads=(), writes=(), sembuf=None):
        key = self._dsem(sembuf)
        self._wait(q, self._deps(reads, writes, skip_sem=key))
        ins = self.E[q].dma_start(out=out, in_=in_)
        self.dsems[key] += 16
        sembuf.dcnt = self.dsems[key]
        ins.then_inc(self.semh[key], 16)
        ev = (key, self.dsems[key])
        for b in reads:
            if b.reads.get(key, 0) < ev[1]:
                b.reads[key] = ev[1]
        for b in writes:
            b.last_w = ev
            b.reads = {}
        return ins

    def barrier(self, engines=("pe", "act", "dve", "sp")):
        evs = [(e, self.cnt[e]) for e in ("pe", "act", "dve") if self.cnt[e] > 0]
        for b in self.local:
            if b.dsem is not None:
                evs.append((b.dsem, self.dsems[b.dsem]))
        for e in engines:
            self._wait(e, evs)

    def begin_phase(self):
        self.stack = contextlib.ExitStack()
        self.local = []
        self.phase_no += 1

    def end_phase(self):
        self.barrier()
        for b in self.local:
            if b.dsem is not None:
                self.free_dsems.append(b.dsem)
        self.stack.close()
        self.stack = None
        self.local = []


class Prog:
    def __init__(self, NT, phases, debug=False):
        self.debug = debug
        self.NT = NT
        self.T = NT * TT
        T = self.T
        nc = bass.Bass("TRN2", target_bir_lowering=False)
        self.nc = nc
        self.k = KB(nc)
        k = self.k
        self.xin = nc.dram_tensor("xT", [D, T], F32, kind="ExternalInput").ap()
        self.memT = nc.dram_tensor("memT", [D, MEM], F32, kind="ExternalInput").ap()
        self.pos = nc.dram_tensor("pos", [128, T], I32, kind="ExternalInput").ap()
        self.vecs_d = nc.dram_tensor("vecs", [128, NV], F32, kind="ExternalInput").ap()
        self.wffn = nc.dram_tensor("wffn", [4, 2, 128, NJH * 3072], F32, kind="ExternalInput").ap()
        self.wxa = nc.dram_tensor("wxa", [4, 128, 16384], F32, kind="ExternalInput").ap()
        self.wkv = nc.dram_tensor("wkv", [4, 128, 16384], F32, kind="ExternalInput").ap()
        self.wmixe = nc.dram_tensor("wmixe", [2, 128, 20992], F32, kind="ExternalInput").ap()
        self.wmla = nc.dram_tensor("wmla", [2, 128, 27136], F32, kind="ExternalInput").ap()
        self.consts_d = nc.dram_tensor("cbf", [128, 1024], F32, kind="ExternalInput").ap()
        self.yout = nc.dram_tensor("yT", [D, T], F32, kind="ExternalOutput").ap()
        self.xres = nc.dram_tensor("xres", [D, T], F32, kind="Internal").ap()
        self.hT = nc.dram_tensor("hTs", [D, T], BF16, kind="Internal").ap()
        self.kvs = nc.dram_tensor("kvs", [4, 128, 4096], BF16, kind="Internal").ap()
        self.qT = nc.dram_tensor("qTs", [16, 96, T], BF16, kind="Internal").ap()
        self.kT = nc.dram_tensor("kTs", [16, 96, T], BF16, kind="Internal").ap()
        self.vS = nc.dram_tensor("vSs", [T, 16 * 65], BF16, kind="Internal").ap()
        self.oT = nc.dram_tensor("oTs", [D, T], BF16, kind="Internal").ap()
        self.ropeC = nc.dram_tensor("ropeC", [128, T], F32, kind="Internal").ap()
        self.ropeS = nc.dram_tensor("ropeS", [128, T], F32, kind="Internal").ap()
        self.d_x = [k.dram("dx%d" % t) for t in range(NT)]
        self.d_h = [k.dram("dh%d" % t) for t in range(NT)]
        self.d_o = [k.dram("do%d" % t) for t in range(NT)]
        self.d_q = [k.dram("dq%d" % t) for t in range(NT)]
        self.d_k = [k.dram("dk%d" % t) for t in range(NT)]
        self.d_v = [k.dram("dv%d" % t) for t in range(NT)]
        self.d_rope = [k.dram("dr%d" % t) for t in range(NT)]
        self.d_kvs = [k.dram("dkvs%d" % l) for l in range(4)]
        self.d_y = k.dram("dy")
        self.W = nc.alloc_sbuf_tensor("Warena", [128, NSLAB * SLAB], BF16)
        self.slabs = [Buf("slab%d" % i, self.W) for i in range(NSLAB)]
        self.vecs = k.sb("s_vecs", [128, NV], F32, persistent=True)
        self.cbf = k.sb("s_cbf", [128, 1024], BF16, persistent=True)
        self.epsb = k.sb("epsb", [128, 1], F32, persistent=True)
        self.pst = nc.alloc_psum_tensor("pst", [128, 4096], F32)
        self.PS = [PBuf("ps%d" % i, self.pst, i * 512) for i in range(8)]
        self.ones32 = k.sb("s_ones32", [128, 128], F32, persistent=True)
        k.op("dve", lambda e: e.memset(self.ones32[:], 1.0), writes=[self.ones32])
        self.psi = 0
        k.dma("sp", self.vecs[:], self.vecs_d[:, :], writes=[self.vecs], sembuf=self.vecs)
        k.dma("pool", self.cbf[:], self.consts_d[:, :], writes=[self.cbf], sembuf=self.cbf)
        k.op("dve", lambda e: e.memset(self.epsb[:], EPS), writes=[self.epsb])
        self.first_x = True
        self.phases = phases
        for pi, ph in enumerate(phases):
            self.last = (pi == len(phases) - 1)
            getattr(self, "ph_" + ph[0])(*ph[1:])
        k._wait("sp", [(s, v) for (s, v) in ([self.d_y.last_w] if self.d_y.last_w else [])])
        k.barrier(engines=("sp",))
        evs = [(key, v) for key, v in k.dsems.items() if v > 0]
        k._wait("sp", evs)

    def ps(self):
        b = self.PS[self.psi]
        self.psi = (self.psi + 1) % 8
        return b

    def wv(self, c0, n):
        return self.W[:, c0:c0 + n], self.slabs[c0 // SLAB:(c0 + n - 1) // SLAB + 1]

    def wload(self, src, c0, n, piece=3072):
        assert c0 % SLAB == 0 and n % SLAB == 0
        o = 0
        while o < n:
            m = min(piece, n - o)
            sl = self.slabs[(c0 + o) // SLAB:(c0 + o + m) // SLAB]
            self.k.dma("pool", self.W[:, c0 + o:c0 + o + m], src[:, o:o + m], writes=sl, sembuf=sl[0])
            o += m

    def vcol(self, name, i=0):
        o = VOFF[name] + i
        return self.vecs[:, o:o + 1]

    def xsrc(self):
        return self.xin if self.first_x else self.xres

    def xtile_ap(self, base, t):
        return base.rearrange("(c p) t -> p c t", p=128)[:, :, t * TT:(t + 1) * TT]

    def load_x(self, xs, t):
        k = self.k
        rd = [] if self.first_x else [self.d_x[t]]
        k.dma("sp", xs[:], self.xtile_ap(self.xsrc(), t), reads=rd, writes=[xs], sembuf=xs)

    def store_x(self, xs, t):
        k = self.k
        if self.last:
            k.dma("sp", self.xtile_ap(self.yout, t), xs[:], reads=[xs], writes=[self.d_y], sembuf=xs)
        else:
            k.dma("sp", self.xtile_ap(self.xres, t), xs[:], reads=[xs], writes=[self.d_x[t]], sembuf=xs)

    def rmsnorm(self, xs, nch, N, gname, h, sq, rstd, tmp, dim):
        k = self.k
        ones = self.cbf[:, 0:128]
        k.op("act", lambda e: e.activation(out=sq[:, 0:nch, 0:N], in_=xs[:, 0:nch, 0:N], func=AF.Square),
             reads=[xs], writes=[sq])
        p = self.ps()
        for c in range(nch):
            k.op("pe", lambda e: e.matmul(p[:, 0:N], ones, sq[:, c, 0:N], start=(c == 0), stop=(c == nch - 1)),
                 reads=[sq, self.cbf], writes=[p], sig=(c == nch - 1))
        k.op("act", lambda e: e.activation(out=tmp[:, 0:N], in_=p[:, 0:N], func=AF.Sqrt, scale=1.0 / dim,
                                           bias=self.epsb[:, 0:1]), reads=[p, self.epsb], writes=[tmp])
        k.op("dve", lambda e: e.reciprocal(rstd[:, 0:N], tmp[:, 0:N]), reads=[tmp], writes=[rstd])
        for c in range(nch):
            k.op("dve", lambda e: e.scalar_tensor_tensor(h[:, c, 0:N], xs[:, c, 0:N], self.vcol(gname, c),
                                                         rstd[:, 0:N], ALU.mult, ALU.mult),
                 reads=[xs, rstd, self.vecs], writes=[h])

    def ph_ffn(self, l, s):
        k = self.k
        NT = self.NT
        k.begin_phase()
        self.wload(self.wffn[l, s], 0, NJH * 3072)
        xs = [k.sb("f_xs%d" % i, [128, KC, TT], F32) for i in range(3)]
        h = [k.sb("f_h%d" % i, [128, KC, TT], BF16) for i in range(2)]
        m = [k.sb("f_m%d" % i, [128, NJH, TT], BF16) for i in range(2)]
        G = [k.sb("f_G%d" % j, [128, TT + 2], F32) for j in range(NJH)]
        rstd = k.sb("f_rstd", [128, TT], F32)
        tmp = k.sb("f_tmp", [128, TT], F32)
        cA = [k.sb("f_cA%d" % i, [128, TT], F32) for i in range(2)]
        cB = [k.sb("f_cB%d" % i, [128, TT], F32) for i in range(2)]
        for j in range(NJH):
            k.op("dve", lambda e: e.memset(G[j][:, 0:2], 0.0), writes=[G[j]])

        def load(t):
            self.load_x(xs[t % 3], t)
            if s == 1:
                k.dma("sp", h[t % 2][:], self.xtile_ap(self.hT, t), reads=[self.d_h[t]], writes=[h[t % 2]],
                      sembuf=h[t % 2])

        def up(t):
            x_ = xs[t % 3]
            h_ = h[t % 2]
            m_ = m[t % 2]
            if s == 0:
                self.rmsnorm(x_, KC, TT, "g_ffn%d" % l, h_, m_, rstd, tmp, D)
                k.dma("sp", self.xtile_ap(self.hT, t), h_[:], reads=[h_], writes=[self.d_h[t]], sembuf=h_)
            for jj in range(NJH):
                j = s * NJH + jj
                c0 = jj * 3072
                pa = self.ps()
                pg = self.ps()
                for (pp, off) in ((pa, 0), (pg, 128)):
                    for kc in range(KC):
                        wap, wsl = self.wv(c0 + kc * 256 + off, 128)
                        k.op("pe", lambda e: e.matmul(pp[:], wap, h_[:, kc, :], start=(kc == 0), stop=(kc == KC - 1)),
                             reads=[h_] + wsl, writes=[pp], sig=(kc == KC - 1))
                g_ = G[jj]
                if t > 0:
                    k.op("dve", lambda e: e.tensor_copy(out=g_[:, 0:2], in_=g_[:, TT:TT + 2]), reads=[g_], writes=[g_])
                k.op("act", lambda e: e.activation(out=g_[:, 2:TT + 2], in_=pg[:], func=AF.Copy), reads=[pg], writes=[g_])
                ca = cA[jj % 2]
                cb = cB[jj % 2]
                cw = lambda i: self.vcol("ffn_cw%d" % l, j * 4 + i)
                k.op("act", lambda e: e.activation(out=ca[:], in_=pg[:], func=AF.Identity, scale=cw(2), bias=cw(3)),
                     reads=[pg, self.vecs], writes=[ca])
                k.op("dve", lambda e: e.scalar_tensor_tensor(cb[:], g_[:, 1:TT + 1], cw(1), ca[:], ALU.mult, ALU.add),
                     reads=[g_, ca, self.vecs], writes=[cb])
                k.op("dve", lambda e: e.scalar_tensor_tensor(ca[:], g_[:, 0:TT], cw(0), cb[:], ALU.mult, ALU.add),
                     reads=[g_, cb, self.vecs], writes=[ca])
                k.op("act", lambda e: e.activation(out=cb[:], in_=ca[:], func=AF.Silu), reads=[ca], writes=[cb])
                k.op("dve", lambda e: e.tensor_tensor(m_[:, jj, :], cb[:], pa[:], ALU.mult), reads=[cb, pa], writes=[m_])

        def down(t):
            x_ = xs[t % 3]
            m_ = m[t % 2]
            for c in range(KC):
                p = self.ps()
                for jj in range(NJH):
                    wap, wsl = self.wv(jj * 3072 + 2048 + c * 128, 128)
                    k.op("pe", lambda e: e.matmul(p[:], wap, m_[:, jj, :], start=(jj == 0), stop=(jj == NJH - 1)),
                         reads=[m_] + wsl, writes=[p], sig=(jj == NJH - 1))
                k.op("dve", lambda e: e.tensor_tensor(x_[:, c, :], x_[:, c, :], p[:], ALU.add), reads=[x_, p], writes=[x_])
            self.store_x(x_, t)

        load(0)
        if NT > 1:
            load(1)
        for t in range(NT + 1):
            if t < NT:
                up(t)
            if t > 0:
                down(t - 1)
            if t + 2 < NT + 1 and t + 2 < NT:
                load(t + 2)
        if s == 1 or True:
            self.first_x = False
        k.end_phase()

    def ph_memkv(self, l):
        k = self.k
        k.begin_phase()
        self.wload(self.wkv[l], 0, 16384, piece=4096)
        ms = k.sb("mk_ms", [128, KC, MEM], F32)
        mn = k.sb("mk_mn", [128, KC, MEM], BF16)
        sq = k.sb("mk_sq", [128, KC, MEM], BF16)
        rstd = k.sb("mk_rstd", [128, TT], F32)
        tmp = k.sb("mk_tmp", [128, TT], F32)
        kv = k.sb("mk_kv", [128, 4096], BF16)
        k.dma("sp", ms[:], self.memT.rearrange("(c p) m -> p c m", p=128), writes=[ms], sembuf=ms)
        self.rmsnorm(ms, KC, MEM, "g_mem%d" % l, mn, sq, rstd, tmp, D)
        for c in range(KC):
            p = self.ps()
            for kc in range(KC):
                wap, wsl = self.wv(kc * 2048 + c * 128, 128)
                k.op("pe", lambda e: e.matmul(p[:, 0:MEM], wap, mn[:, kc, :], start=(kc == 0), stop=(kc == KC - 1)),
                     reads=[mn] + wsl, writes=[p], sig=(kc == KC - 1))
            k.op("act", lambda e: e.activation(out=kv[:, c * MEM:(c + 1) * MEM], in_=p[:, 0:MEM], func=AF.Copy,
                                               scale=1.0 / 16.0), reads=[p], writes=[kv])
        for mc in range(2):
            for hf in range(2):
                p = self.ps()
                for kc in range(KC):
                    wap, wsl = self.wv(kc * 2048 + 1024 + hf * 512, 512)
                    k.op("pe", lambda e: e.matmul(p[:], mn[:, kc, mc * 128:(mc + 1) * 128], wap, start=(kc == 0),
                                                  stop=(kc == KC - 1)), reads=[mn] + wsl, writes=[p], sig=(kc == KC - 1))
                o = 2048 + mc * 1024 + hf * 512
                k.op("dve", lambda e: e.tensor_copy(out=kv[:, o:o + 512], in_=p[:]), reads=[p], writes=[kv])
        k.dma("sp", self.kvs[l], kv[:], reads=[kv], writes=[self.d_kvs[l]], sembuf=kv)
        k.end_phase()

    def ph_xa(self, l):
        k = self.k
        NT = self.NT
        k.begin_phase()
        self.wload(self.wxa[l], 0, 16384, piece=4096)
        kv = k.sb("xa_kv", [128, 4096], BF16)
        k.dma("sp", kv[:], self.kvs[l], reads=[self.d_kvs[l]], writes=[kv], sembuf=kv)
        xs = [k.sb("xa_xs%d" % i, [128, KC, TT], F32) for i in range(2)]
        h = k.sb("xa_h", [128, KC, TT], BF16)
        sq = k.sb("xa_sq", [128, KC, TT], BF16)
        q = k.sb("xa_q", [128, KC, TT], BF16)
        o = k.sb("xa_o", [128, KC, TT], BF16)
        P = [k.sb("xa_P%d" % i, [128, 2, TT], BF16) for i in range(2)]
        rden = [k.sb("xa_rden%d" % i, [128, TT], F32) for i in range(2)]
        rstd = k.sb("xa_rstd", [128, TT], F32)
        tmp = k.sb("xa_tmp", [128, TT], F32)
        ones = self.cbf[:, 0:128]
        self.load_x(xs[0], 0)
        for t in range(NT):
            x_ = xs[t % 2]
            if t + 1 < NT:
                self.load_x(xs[(t + 1) % 2], t + 1)
            self.rmsnorm(x_, KC, TT, "g_xa%d" % l, h, sq, rstd, tmp, D)
            for c in range(KC):
                p = self.ps()
                for kc in range(KC):
                    wap, wsl = self.wv(kc * 1024 + c * 128, 128)
                    k.op("pe", lambda e: e.matmul(p[:], wap, h[:, kc, :], start=(kc == 0), stop=(kc == KC - 1)),
                         reads=[h] + wsl, writes=[p], sig=(kc == KC - 1))
                if c % 2 == 0:
                    k.op("act", lambda e: e.activation(out=q[:, c, :], in_=p[:], func=AF.Copy), reads=[p], writes=[q])
                else:
                    k.op("dve", lambda e: e.tensor_copy(out=q[:, c, :], in_=p[:]), reads=[p], writes=[q])
            for hd in range(4):
                P_ = P[hd % 2]
                rd = rden[hd % 2]
                for mc in range(2):
                    p = self.ps()
                    for dc in range(2):
                        c0 = (2 * hd + dc) * MEM + mc * 128
                        k.op("pe", lambda e: e.matmul(p[:], kv[:, c0:c0 + 128], q[:, 2 * hd + dc, :], start=(dc == 0),
                                                      stop=(dc == 1)), reads=[kv, q], writes=[p], sig=(dc == 1))
                    k.op("act", lambda e: e.activation(out=P_[:, mc, :], in_=p[:], func=AF.Exp), reads=[p], writes=[P_])
                pd = self.ps()
                for mc in range(2):
                    k.op("pe", lambda e: e.matmul(pd[:], ones, P_[:, mc, :], start=(mc == 0), stop=(mc == 1)),
                         reads=[P_, self.cbf], writes=[pd], sig=(mc == 1))
                k.op("dve", lambda e: e.reciprocal(rd[:], pd[:]), reads=[pd], writes=[rd])
                for dc in range(2):
                    p = self.ps()
                    for mc in range(2):
                        c0 = 2048 + mc * 1024 + hd * 256 + dc * 128
                        k.op("pe", lambda e: e.matmul(p[:], kv[:, c0:c0 + 128], P_[:, mc, :], start=(mc == 0),
                                                      stop=(mc == 1)), reads=[kv, P_], writes=[p], sig=(mc == 1))
                    k.op("dve", lambda e: e.tensor_tensor(o[:, 2 * hd + dc, :], p[:], rd[:], ALU.mult),
                         reads=[p, rd], writes=[o])
            for c in range(KC):
                p = self.ps()
                for kc in range(KC):
                    wap, wsl = self.wv(8192 + kc * 1024 + c * 128, 128)
                    k.op("pe", lambda e: e.matmul(p[:], wap, o[:, kc, :], start=(kc == 0), stop=(kc == KC - 1)),
                         reads=[o] + wsl, writes=[p], sig=(kc == KC - 1))
                k.op("dve", lambda e: e.tensor_tensor(x_[:, c, :], x_[:, c, :], p[:], ALU.add), reads=[x_, p], writes=[x_])
            self.store_x(x_, t)
        self.first_x = False
        k.end_phase()

    def ph_mixe(self, l):
        k = self.k
        NT = self.NT
        e = l // 2
        k.begin_phase()
        WIN, WPOOL, WOUT, WDIAG = 0, 12288, 12800, 20992
        self.wload(self.wmixe[e], 0, 20992, piece=4096)
        ident = self.cbf[:, 128:256]
        for ci in range(4):
            for j in range(31):
                c0 = WDIAG + (ci * 31 + j) * 128
                k.op("dve", lambda e_: e_.tensor_scalar_mul(self.W[:, c0:c0 + 128], ident, self.vcol("dw_w%d" % e, ci * 31 + j)),
                     reads=[self.cbf, self.vecs], writes=[self.slabs[c0 // SLAB]])
        xs = [k.sb("mx_xs%d" % i, [128, KC, TT], F32) for i in range(2)]
        h = k.sb("mx_h", [128, KC, TT], BF16)
        sq = k.sb("mx_sq", [128, KC, TT], BF16)
        U = [k.sb("mx_U%d" % g, [128, TT + 15], F32) for g in range(4)]
        AB = [k.sb("mx_AB%d" % i, [128, TT + 15], F32) for i in range(2)]
        pooled = [k.sb("mx_pl%d" % i, [128, TT], BF16) for i in range(2)]
        t16 = k.sb("mx_t16", [128, 16], F32)
        cat = k.sb("mx_cat", [128, KC, TT], BF16)
        sig = [k.sb("mx_sig%d" % i, [128, TT], F32) for i in range(2)]
        GL = [k.sb("mx_GL%d" % c, [128, TT + 30], BF16) for c in range(4)]
        CV = k.sb("mx_CV", [128, 4, TT], F32)
        XC = k.sb("mx_XC", [128, 4, TT], F32)
        SQ = k.sb("mx_SQ", [128, 4, TT], F32)
        rstd = k.sb("mx_rstd", [128, TT], F32)
        tmp = k.sb("mx_tmp", [128, TT], F32)
        lnr = k.sb("mx_lnr", [128, TT], F32)
        lnt = k.sb("mx_lnt", [128, TT], F32)
        yt = [k.sb("mx_yt%d" % i, [128, TT], F32) for i in range(2)]
        for g in range(4):
            k.op("dve", lambda e_: e_.memset(U[g][:, 0:15], 0.0), writes=[U[g]])
            k.op("dve", lambda e_: e_.memset(GL[g][:, 0:30], 0.0), writes=[GL[g]])
        self.load_x(xs[0], 0)
        for t in range(NT):
            x_ = xs[t % 2]
            if t + 1 < NT:
                self.load_x(xs[(t + 1) % 2], t + 1)
            self.rmsnorm(x_, KC, TT, "g_mix%d" % l, h, sq, rstd, tmp, D)

            def zmm(c):
                p = self.ps()
                for kc in range(KC):
                    wap, wsl = self.wv(WIN + kc * 1536 + c * 128, 128)
                    k.op("pe", lambda e_: e_.matmul(p[:], wap, h[:, kc, :], start=(kc == 0), stop=(kc == KC - 1)),
                         reads=[h] + wsl, writes=[p], sig=(kc == KC - 1))
                return p
            for g in range(4):
                w = 2 << g
                lvl = g + 1
                p = zmm(g)
                u = U[g]
                if t > 0:
                    k.op("dve", lambda e_: e_.tensor_copy(out=u[:, 0:15], in_=u[:, TT:TT + 15]), reads=[u], writes=[u])
                k.op("act", lambda e_: e_.activation(out=u[:, 15:TT + 15], in_=p[:], func=AF.Copy), reads=[p], writes=[u])
                need = [0] * (lvl + 1)
                need[lvl] = 15
                for kk in range(lvl, 0, -1):
                    need[kk - 1] = need[kk] - (1 << (kk - 1))
                src = u
                for kk in range(1, lvl + 1):
                    dst = AB[kk % 2]
                    sh = 1 << (kk - 1)
                    a0 = need[kk]
                    k.op("dve", lambda e_: e_.tensor_tensor(dst[:, a0:TT + 15], src[:, a0:TT + 15],
                                                            src[:, a0 - sh:TT + 15 - sh], ALU.add),
                         reads=[src], writes=[dst])
                    src = dst
                pl = pooled[g % 2]
                k.op("dve", lambda e_: e_.scalar_tensor_tensor(pl[:], src[:, 15:TT + 15], 1.0 / w, u[:, 15:TT + 15],
                                                               ALU.mult, ALU.subtract), reads=[src, u], writes=[pl])
                if t == 0:
                    o_ = VOFF["invcnt"] + g * 16
                    k.op("dve", lambda e_: e_.tensor_tensor(t16[:], src[:, 15:31], self.vecs[:, o_:o_ + 16], ALU.mult),
                         reads=[src, self.vecs], writes=[t16])
                    k.op("dve", lambda e_: e_.tensor_tensor(pl[:, 0:16], t16[:], u[:, 15:31], ALU.subtract),
                         reads=[t16, u], writes=[pl])
                py = self.ps()
                wap, wsl = self.wv(WPOOL + g * 128, 128)
                k.op("pe", lambda e_: e_.matmul(py[:], wap, pl[:], start=True, stop=True), reads=[pl] + wsl, writes=[py])
                k.op("act", lambda e_: e_.activation(out=cat[:, g, :], in_=py[:], func=AF.Copy,
                                                     scale=self.vcol("pool_scale%d" % e, g)),
                     reads=[py, self.vecs], writes=[cat])
            for ci in range(4):
                pa = zmm(4 + ci)
                pb = zmm(8 + ci)
                sg = sig[ci % 2]
                gl = GL[ci]
                k.op("act", lambda e_: e_.activation(out=sg[:], in_=pb[:], func=AF.Sigmoid), reads=[pb], writes=[sg])
                if t > 0:
                    k.op("dve", lambda e_: e_.tensor_copy(out=gl[:, 0:30], in_=gl[:, TT:TT + 30]), reads=[gl], writes=[gl])
                k.op("dve", lambda e_: e_.tensor_tensor(gl[:, 30:TT + 30], sg[:], pa[:], ALU.mult), reads=[sg, pa], writes=[gl])
                pc = self.ps()
                for j in range(31):
                    wap, wsl = self.wv(WDIAG + (ci * 31 + j) * 128, 128)
                    k.op("pe", lambda e_: e_.matmul(pc[:], wap, gl[:, j:j + TT], start=(j == 0), stop=(j == 30)),
                         reads=[gl] + wsl, writes=[pc], sig=(j == 30))
                k.op("act", lambda e_: e_.activation(out=CV[:, ci, :], in_=pc[:], func=AF.Identity,
                                                     bias=self.vcol("dw_b%d" % e, ci)), reads=[pc, self.vecs], writes=[CV])
            pm = self.ps()
            for ci in range(4):
                k.op("pe", lambda e_: e_.matmul(pm[:], self.ones32[:], CV[:, ci, :], start=(ci == 0), stop=(ci == 3)),
                     reads=[CV, self.ones32], writes=[pm], sig=(ci == 3))
            for ci in range(4):
                k.op("dve", lambda e_: e_.scalar_tensor_tensor(XC[:, ci, :], pm[:], -1.0 / 512.0, CV[:, ci, :],
                                                               ALU.mult, ALU.add), reads=[pm, CV], writes=[XC])
            k.op("act", lambda e_: e_.activation(out=SQ[:], in_=XC[:], func=AF.Square), reads=[XC], writes=[SQ])
            pv = self.ps()
            for ci in range(4):
                k.op("pe", lambda e_: e_.matmul(pv[:], self.ones32[:], SQ[:, ci, :], start=(ci == 0), stop=(ci == 3)),
                     reads=[SQ, self.ones32], writes=[pv], sig=(ci == 3))
            k.op("act", lambda e_: e_.activation(out=lnt[:], in_=pv[:], func=AF.Sqrt, scale=1.0 / 512.0,
                                                 bias=self.epsb[:, 0:1]), reads=[pv, self.epsb], writes=[lnt])
            k.op("dve", lambda e_: e_.reciprocal(lnr[:], lnt[:]), reads=[lnt], writes=[lnr])
            for ci in range(4):
                y_ = yt[ci % 2]
                k.op("dve", lambda e_: e_.scalar_tensor_tensor(y_[:], XC[:, ci, :], self.vcol("ln_g%d" % e, ci), lnr[:],
                                                               ALU.mult, ALU.mult), reads=[XC, lnr, self.vecs], writes=[y_])
                k.op("act", lambda e_: e_.activation(out=cat[:, 4 + ci, :], in_=y_[:], func=AF.Silu,
                                                     bias=self.vcol("ln_b%d" % e, ci)), reads=[y_, self.vecs], writes=[cat])
            for c in range(KC):
                p = self.ps()
                for kc in range(KC):
                    wap, wsl = self.wv(WOUT + kc * 1024 + c * 128, 128)
                    k.op("pe", lambda e_: e_.matmul(p[:], wap, cat[:, kc, :], start=(kc == 0), stop=(kc == KC - 1)),
                         reads=[cat] + wsl, writes=[p], sig=(kc == KC - 1))
                k.op("dve", lambda e_: e_.tensor_tensor(x_[:, c, :], x_[:, c, :], p[:], ALU.add), reads=[x_, p], writes=[x_])
            self.store_x(x_, t)
        self.first_x = False
        k.end_phase()

    def ph_rope(self):
        k = self.k
        k.begin_phase()
        import math
        MAGIC = 8388608.0
        C1 = 6.28125
        C2 = 2.0 * math.pi - 6.28125
        PIS = 3.1415925
        nb = lambda n: [k.sb("rp_%s%d" % (n, i), [128, TT], F32) for i in range(2)]
        posi = [k.sb("rp_posi%d" % i, [128, TT], I32) for i in range(2)]
        posf, ang, kf, kr, r1, r2, r3, sn, ss, ab, cs = (nb(n) for n in
                                                         ("posf", "ang", "kf", "kr", "r1", "r2", "r3", "sn", "ss", "ab", "cs"))
        hpi = k.sb("rp_hpi", [128, 1], F32)
        k.op("dve", lambda e: e.memset(hpi[:], math.pi / 2.0), writes=[hpi])
        for t in range(self.NT):
            i = t % 2
            k.dma("sp", posi[i][:], self.pos[:, t * TT:(t + 1) * TT], writes=[posi[i]],
                  sembuf=posi[i])
            k.op("dve", lambda e: e.tensor_copy(out=posf[i][:], in_=posi[i][:]), reads=[posi[i]], writes=[posf[i]])
            k.op("dve", lambda e: e.tensor_scalar_mul(ang[i][:], posf[i][:], self.vcol("inv128")),
                 reads=[posf[i], self.vecs], writes=[ang[i]])
            k.op("dve", lambda e: e.tensor_scalar(kf[i][:], ang[i][:], 1.0 / (2.0 * math.pi), MAGIC, ALU.mult, ALU.add),
                 reads=[ang[i]], writes=[kf[i]])
            k.op("dve", lambda e: e.tensor_scalar_add(kr[i][:], kf[i][:], -MAGIC), reads=[kf[i]], writes=[kr[i]])
            k.op("dve", lambda e: e.scalar_tensor_tensor(r1[i][:], kr[i][:], -C1, ang[i][:], ALU.mult, ALU.add),
                 reads=[kr[i], ang[i]], writes=[r1[i]])
            k.op("dve", lambda e: e.scalar_tensor_tensor(r2[i][:], kr[i][:], -C2, r1[i][:], ALU.mult, ALU.add),
                 reads=[kr[i], r1[i]], writes=[r2[i]])
            k.op("dve", lambda e: e.tensor_scalar(r3[i][:], r2[i][:], -PIS, PIS, ALU.max, ALU.min),
                 reads=[r2[i]], writes=[r3[i]])
            k.op("act", lambda e: e.activation(out=sn[i][:], in_=r3[i][:], func=AF.Sin), reads=[r3[i]], writes=[sn[i]])
            k.op("dve", lambda e: e.tensor_scalar_mul(ss[i][:], sn[i][:], self.vcol("sgn128")),
                 reads=[sn[i], self.vecs], writes=[ss[i]])
            k.op("act", lambda e: e.activation(out=ab[i][:], in_=r3[i][:], func=AF.Abs), reads=[r3[i]], writes=[ab[i]])
            k.op("act", lambda e: e.activation(out=cs[i][:], in_=ab[i][:], func=AF.Sin, scale=-1.0, bias=hpi[:, 0:1]),
                 reads=[ab[i], hpi], writes=[cs[i]])
            k.dma("sp", self.ropeC[:, t * TT:(t + 1) * TT], cs[i][:], reads=[cs[i]], writes=[self.d_rope[t]], sembuf=cs[i])
            k.dma("sp", self.ropeS[:, t * TT:(t + 1) * TT], ss[i][:], reads=[ss[i]], writes=[self.d_rope[t]], sembuf=ss[i])
        k.end_phase()

    def ph_mla1(self, l):
        k = self.k
        NT = self.NT
        o = l // 2
        k.begin_phase()
        WA, WB, WC = 0, 6144, 13824
        self.wload(self.wmla[o][:, 0:18944], 0, 18944, piece=4096)
        SCALE = 1.0 / (96.0 ** 0.5)
        xs = [k.sb("m1_xs%d" % i, [128, KC, TT], F32) for i in range(2)]
        rc = [k.sb("m1_rc%d" % i, [128, TT], F32) for i in range(2)]
        rs = [k.sb("m1_rs%d" % i, [128, TT], F32) for i in range(2)]
        h = k.sb("m1_h", [128, KC, TT], BF16)
        sq = k.sb("m1_sq", [128, KC, TT], BF16)
        rstd = k.sb("m1_rstd", [128, TT], F32)
        tmp = k.sb("m1_tmp", [128, TT], F32)
        CQ = k.sb("m1_CQ", [128, 3, TT], F32)
        CKV = k.sb("m1_CKV", [128, 2, TT], F32)
        cqn = k.sb("m1_cqn", [128, 3, TT], BF16)
        ckvn = k.sb("m1_ckvn", [128, 2, TT], BF16)
        t1 = [k.sb("m1_t1%d" % i, [128, TT], F32) for i in range(2)]
        t2 = [k.sb("m1_t2%d" % i, [128, TT], F32) for i in range(2)]
        KR = k.sb("m1_KR", [128, TT], BF16)
        QPE = k.sb("m1_QPE", [128, 4, TT], BF16)
        QS = k.sb("m1_QS", [128, 16, TT], BF16)
        KS = k.sb("m1_KS", [128, 16, TT], BF16)
        VS = k.sb("m1_VS", [128, 4, 16, 65], BF16)
        k.op("dve", lambda e: e.memset(VS[:], 1.0), writes=[VS])

        def load(t):
            i = t % 2
            self.load_x(xs[i], t)
            k.dma("sp", rc[i][:], self.ropeC[:, t * TT:(t + 1) * TT], reads=[self.d_rope[t]], writes=[rc[i]], sembuf=rc[i])
            k.dma("sp", rs[i][:], self.ropeS[:, t * TT:(t + 1) * TT], reads=[self.d_rope[t]], writes=[rs[i]], sembuf=rs[i])

        def mm(p, M, c0, nkc, stride, rhs_buf, extra=None):
            for kc in range(nkc):
                wap, wsl = self.wv(c0 + kc * stride, M)
                last = (kc == nkc - 1) and extra is None
                k.op("pe", lambda e: e.matmul(p[0:M, :], wap, rhs_buf[:, kc, :], start=(kc == 0), stop=last),
                     reads=[rhs_buf] + wsl, writes=[p], sig=last)
            if extra is not None:
                lhsT, rhs, rb = extra
                k.op("pe", lambda e: e.matmul(p[0:M, :], lhsT, rhs, start=False, stop=True),
                     reads=[rb, self.cbf], writes=[p])

        def rope(pa, pb, np_, i, out_ap, out_buf, ti):
            a, b = t1[ti], t2[ti]
            k.op("dve", lambda e: e.tensor_tensor(a[0:np_, :], pa[0:np_, :], rc[i][0:np_, :], ALU.mult),
                 reads=[pa, rc[i]], writes=[a])
            k.op("dve", lambda e: e.tensor_tensor(b[0:np_, :], pb[0:np_, :], rs[i][0:np_, :], ALU.mult),
                 reads=[pb, rs[i]], writes=[b])
            k.op("dve", lambda e: e.tensor_tensor(out_ap, a[0:np_, :], b[0:np_, :], ALU.add), reads=[a, b], writes=[out_buf])

        load(0)
        for t in range(NT):
            i = t % 2
            x_ = xs[i]
            if t + 1 < NT:
                load(t + 1)
            self.rmsnorm(x_, KC, TT, "g_mix%d" % l, h, sq, rstd, tmp, D)
            for c in range(3):
                p = self.ps()
                mm(p, 128, WA + c * 128, KC, 768, h)
                k.op("act", lambda e: e.activation(out=CQ[:, c, :], in_=p[:], func=AF.Copy), reads=[p], writes=[CQ])
            for c in range(2):
                p = self.ps()
                mm(p, 128, WA + 384 + c * 128, KC, 768, h)
                k.op("dve", lambda e: e.tensor_copy(out=CKV[:, c, :], in_=p[:]), reads=[p], writes=[CKV])
            if getattr(self, "debug", False):
                k.dma("sp", self.xtile_ap(self.oT, t), h[:], reads=[h], writes=[self.d_o[t]], sembuf=h)
            p1 = self.ps()
            mm(p1, 32, WA + 640, KC, 768, h)
            p2 = self.ps()
            mm(p2, 32, WA + 672, KC, 768, h)
            rope(p1, p2, 32, i, KR[0:32, :], KR, 0)
            self.rmsnorm(CQ, 3, TT, "g_q%d" % o, cqn, sq, rstd, tmp, 384)
            self.rmsnorm(CKV, 2, TT, "g_kv%d" % o, ckvn, sq, rstd, tmp, 256)
            for c in range(4):
                pa = self.ps()
                mm(pa, 128, WB + c * 128, 3, 2560, cqn)
                pb = self.ps()
                mm(pb, 128, WB + 512 + c * 128, 3, 2560, cqn)
                rope(pa, pb, 128, i, QPE[:, c, :], QPE, c % 2)
            for hh in range(16):
                p = self.ps()
                sel = self.cbf[:, 384 + (hh % 4) * 96:384 + (hh % 4) * 96 + 96]
                mm(p, 96, WB + 1024 + hh * 96, 3, 2560, cqn, extra=(sel, QPE[:, hh // 4, :], QPE))
                if hh % 2 == 0:
                    k.op("act", lambda e: e.activation(out=QS[0:96, hh, :], in_=p[0:96, :], func=AF.Copy, scale=SCALE),
                         reads=[p], writes=[QS])
                else:
                    k.op("dve", lambda e: e.tensor_scalar_mul(QS[0:96, hh, :], p[0:96, :], SCALE), reads=[p], writes=[QS])
            for hh in range(16):
                p = self.ps()
                selk = self.cbf[0:32, 768:864]
                mm(p, 96, WC + hh * 96, 2, 2560, ckvn, extra=(selk, KR[0:32, :], KR))
                if hh % 2 == 1:
                    k.op("act", lambda e: e.activation(out=KS[0:96, hh, :], in_=p[0:96, :], func=AF.Copy), reads=[p], writes=[KS])
                else:
                    k.op("dve", lambda e: e.tensor_copy(out=KS[0:96, hh, :], in_=p[0:96, :]), reads=[p], writes=[KS])
            for ts in range(4):
                for hf in range(2):
                    p = self.ps()
                    for kc in range(2):
                        wap, wsl = self.wv(WC + kc * 2560 + 1536 + hf * 512, 512)
                        k.op("pe", lambda e: e.matmul(p[:], ckvn[:, kc, ts * 128:(ts + 1) * 128], wap, start=(kc == 0),
                                                      stop=(kc == 1)), reads=[ckvn] + wsl, writes=[p], sig=(kc == 1))
                    src = p[:].rearrange("p (h d) -> p h d", d=64)
                    if (ts + hf) % 2 == 0:
                        k.op("act", lambda e: e.activation(out=VS[:, ts, hf * 8:(hf + 1) * 8, 0:64], in_=src, func=AF.Copy),
                             reads=[p], writes=[VS])
                    else:
                        k.op("dve", lambda e: e.tensor_copy(out=VS[:, ts, hf * 8:(hf + 1) * 8, 0:64], in_=src),
                             reads=[p], writes=[VS])
            k.dma("sp", self.qT.rearrange("h p t -> p h t")[:, :, t * TT:(t + 1) * TT], QS[0:96, :, :], reads=[QS],
                  writes=[self.d_q[t]], sembuf=QS)
            k.dma("sp", self.kT.rearrange("h p t -> p h t")[:, :, t * TT:(t + 1) * TT], KS[0:96, :, :], reads=[KS],
                  writes=[self.d_k[t]], sembuf=KS)
            k.dma("sp", self.vS.rearrange("(n s p) (h d) -> n p s h d", s=4, p=128, d=65)[t], VS[:], reads=[VS],
                  writes=[self.d_v[t]], sembuf=VS)
        k.end_phase()

    def ph_mla2(self, l):
        k = self.k
        NT = self.NT
        T = self.T
        NKC = T // 128
        k.begin_phase()
        KT = [k.sb("m2_KT%d" % i, [128, T], BF16) for i in range(2)]
        QT = [k.sb("m2_QT%d" % i, [128, T], BF16) for i in range(2)]
        VV = [k.sb("m2_VV%d" % i, [128, NKC, 65], BF16) for i in range(2)]
        rden = [k.sb("m2_rden%d" % i, [128, TT], F32) for i in range(2)]
        bcs = [k.sb("m2_bcs%d" % i, [128, TT], F32) for i in range(2)]
        ONt = [k.sb("m2_ON%d" % i, [128, TT], BF16) for i in range(2)]
        tri = self.cbf[:, 256:384]
        allq = list(self.d_q)
        allk = list(self.d_k)
        allv = list(self.d_v)

        def loadh(hh):
            i = hh % 2
            k.dma("sp", KT[i][0:96, :], self.kT[hh], reads=allk, writes=[KT[i]], sembuf=KT[i])
            k.dma("sp", QT[i][0:96, :], self.qT[hh], reads=allq, writes=[QT[i]], sembuf=QT[i])
            k.dma("sp", VV[i][:], self.vS.rearrange("(c p) f -> p c f", p=128)[:, :, hh * 65:(hh + 1) * 65],
                  reads=allv, writes=[VV[i]], sembuf=VV[i])

        PT = [k.sb("m2_PTr%d" % i, [128, 3, TT], BF16) for i in range(4)]
        groups = []
        gi = 0
        ti = 0
        for hh in range(16):
            for j in range(NT):
                chunks = [(kc, 0) for kc in range(4 * j)] + [(4 * j + d, 128 * d) for d in range(4)]
                glist = []
                pos_ = 0
                while pos_ < len(chunks):
                    n = 3 if gi % 2 == 0 else 2
                    glist.append((gi, chunks[pos_:pos_ + n]))
                    pos_ += n
                    gi += 1
                for idx, (g, cl) in enumerate(glist):
                    groups.append(dict(g=g, cl=cl, hh=hh, j=j, ti=ti, first=(idx == 0), last=(idx == len(glist) - 1),
                                       first_of_head=(j == 0 and idx == 0)))
                ti += 1
        ident = self.cbf[:, 128:256]
        mneg = self.cbf[:, 864:992]

        def stage1(G):
            g, cl, hh, j = G["g"], G["cl"], G["hh"], G["j"]
            kt, qt = KT[hh % 2], QT[hh % 2]
            b0 = 0 if g % 2 == 0 else 3
            n = len(cl)
            banks = [self.PS[b0 + ii] for ii in range(n)]
            pt = PT[g % 4]
            for ii, (kc, c_lo) in enumerate(cl):
                isdiag = kc >= 4 * j
                k.op("pe", lambda e: e.matmul(banks[ii][:, c_lo:TT], kt[0:96, kc * 128:(kc + 1) * 128],
                                              qt[0:96, j * TT + c_lo:(j + 1) * TT], start=True, stop=not isdiag),
                     reads=[kt, qt], writes=[banks[ii]], sig=(ii == n - 1 and not isdiag))
                if isdiag:
                    k.op("pe", lambda e: e.matmul(banks[ii][:, c_lo:c_lo + 128], ident, mneg, start=False, stop=True),
                         reads=[self.cbf], writes=[banks[ii]], sig=(ii == n - 1))
            k.op("act", lambda e: e.activation(out=pt[:, 0:n, :], in_=self.pst[:, b0 * 512:(b0 + n) * 512]
                                               .rearrange("p (n t) -> p n t", t=TT), func=AF.Exp),
                 reads=banks, writes=[pt])

        def stage2(G):
            g, cl, hh = G["g"], G["cl"], G["hh"]
            vv = VV[hh % 2]
            Ob = self.PS[OBANK[G["ti"] % 2]]
            pt = PT[g % 4]
            n = len(cl)
            for ii, (kc, c_lo) in enumerate(cl):
                k.op("pe", lambda e: e.matmul(Ob[0:65, c_lo:TT], vv[:, kc, 0:65], pt[:, ii, c_lo:TT],
                                              start=(G["first"] and ii == 0), stop=(G["last"] and ii == n - 1)),
                     reads=[vv, pt], writes=[Ob], sig=(ii == n - 1))

        def fin_a(G):
            f = G["ti"] % 2
            Ob = self.PS[OBANK[f]]
            k.op("dve", lambda e: e.reciprocal(rden[f][64:65, :], Ob[64:65, :]), reads=[Ob], writes=[rden[f]])

        def fin_b(G):
            f = G["ti"] % 2
            hh, j = G["hh"], G["j"]
            Ob = self.PS[OBANK[f]]
            rd, bc, on = rden[f], bcs[f], ONt[f]
            pb = self.PS[BCBANK]
            k.op("pe", lambda e: e.matmul(pb[0:64, :], self.ones32[64:65, 0:64], rd[64:65, :], start=True, stop=True),
                 reads=[rd, self.ones32], writes=[pb])
            k.op("act", lambda e: e.activation(out=bc[0:64, :], in_=pb[0:64, :], func=AF.Copy), reads=[pb], writes=[bc])
            k.op("dve", lambda e: e.tensor_tensor(on[0:64, :], Ob[0:64, :], bc[0:64, :], ALU.mult),
                 reads=[Ob, bc], writes=[on])
            k.dma("sp", self.oT[hh * 64:(hh + 1) * 64, j * TT:(j + 1) * TT], on[0:64, :], reads=[on],
                  writes=[self.d_o[j]], sembuf=on)

        loadh(0)
        prev = None
        deferred = None
        for G in groups:
            stage1(G)
            if deferred is not None:
                fin_b(deferred)
                deferred = None
            if prev is not None:
                stage2(prev)
                if prev["last"]:
                    fin_a(prev)
                    if NODEFER:
                        fin_b(prev)
                    else:
                        deferred = prev
            if G["first_of_head"] and G["hh"] + 1 < 16:
                loadh(G["hh"] + 1)
            prev = G
        if deferred is not None:
            fin_b(deferred)
        stage2(prev)
        fin_a(prev)
        fin_b(prev)
        k.end_phase()

    def ph_mla3(self, l):
        k = self.k
        NT = self.NT
        o = l // 2
        k.begin_phase()
        WD = 18944
        self.wload(self.wmla[o][:, 18944:27136], WD, 8192, piece=4096)
        xs = [k.sb("m3_xs%d" % i, [128, KC, TT], F32) for i in range(2)]
        ot = [k.sb("m3_ot%d" % i, [128, KC, TT], BF16) for i in range(2)]

        def load(t):
            self.load_x(xs[t % 2], t)
            k.dma("sp", ot[t % 2][:], self.xtile_ap(self.oT, t), reads=[self.d_o[t]], writes=[ot[t % 2]], sembuf=ot[t % 2])
        load(0)
        for t in range(NT):
            x_ = xs[t % 2]
            o_ = ot[t % 2]
            if t + 1 < NT:
                load(t + 1)
            for c in range(KC):
                p = self.ps()
                for kc in range(KC):
                    wap, wsl = self.wv(WD + kc * 1024 + c * 128, 128)
                    k.op("pe", lambda e: e.matmul(p[:], wap, o_[:, kc, :], start=(kc == 0), stop=(kc == KC - 1)),
                         reads=[o_] + wsl, writes=[p], sig=(kc == KC - 1))
                k.op("dve", lambda e: e.tensor_tensor(x_[:, c, :], x_[:, c, :], p[:], ALU.add), reads=[x_, p], writes=[x_])
            self.store_x(x_, t)
        self.first_x = False
        k.end_phase()

    def ph_final(self):
        k = self.k
        NT = self.NT
        k.begin_phase()
        xs = [k.sb("fn_xs%d" % i, [128, KC, TT], F32) for i in range(2)]
        ys = [k.sb("fn_ys%d" % i, [128, KC, TT], F32) for i in range(2)]
        sq = k.sb("fn_sq", [128, KC, TT], BF16)
        rstd = k.sb("fn_rstd", [128, TT], F32)
        tmp = k.sb("fn_tmp", [128, TT], F32)
        self.load_x(xs[0], 0)
        for t in range(NT):
            if t + 1 < NT:
                self.load_x(xs[(t + 1) % 2], t + 1)
            y_ = ys[t % 2]
            self.rmsnorm(xs[t % 2], KC, TT, "g_final", y_, sq, rstd, tmp, D)
            k.dma("sp", self.xtile_ap(self.yout, t), y_[:], reads=[y_], writes=[self.d_y], sembuf=y_)
        k.end_phase()

    def ph_dump(self):
        k = self.k
        T = self.T
        k.begin_phase()
        allr = self.d_rope + self.d_q + self.d_k + self.d_v + self.d_o
        a = k.sb("dbg_a", [128, T], BF16)
        b = k.sb("dbg_b", [128, T], F32)
        k.dma("sp", self.yout[0:128, :], self.ropeC[:, :], reads=allr, writes=[self.d_y], sembuf=a)
        k.dma("sp", self.yout[128:256, :], self.ropeS[:, :], reads=allr, writes=[self.d_y], sembuf=a)
        def cp(src, n, r0):
            k.dma("sp", a[0:n, :], src, reads=allr, writes=[a], sembuf=a)
            k.op("dve", lambda e: e.tensor_copy(out=b[0:n, :], in_=a[0:n, :]), reads=[a], writes=[b])
            k.dma("sp", self.yout[r0:r0 + n, :], b[0:n, :], reads=[b], writes=[self.d_y], sembuf=b)
        cp(self.qT[0], 96, 256)
        cp(self.kT[0], 96, 352)
        cp(self.vS[0:128, 0:1024], 128, 448) if T <= 1024 else None
        cp(self.oT[0:128, :], 128, 576)
        cp(self.oT[128:256, :], 128, 704)
        cp(self.qT[5], 96, 832)
        cp(self.kT[5], 96, 928)
        k.end_phase()


def _fm(v):
    v = np.asarray(v, np.float32)
    return np.ascontiguousarray(v.reshape(-1, 128).T)


def prep_shared(inp):
    f32 = np.float32
    vecs = np.zeros((128, NV), f32)

    def put(name, arr):
        arr = np.asarray(arr, f32)
        vecs[:, VOFF[name]:VOFF[name] + arr.shape[1]] = arr

    for l in range(4):
        put("g_mix%d" % l, _fm(inp["norm_mix_g"][l]))
        put("g_xa%d" % l, _fm(inp["norm_xa_g"][l]))
        put("g_mem%d" % l, _fm(inp["norm_mem_g"][l]))
        put("g_ffn%d" % l, _fm(inp["norm_ffn_g"][l]))
        cw = np.asarray(inp["ffn_conv_w"][l], f32)
        cb = np.asarray(inp["ffn_conv_b"][l], f32)
        a = np.stack([cw[0], cw[1], cw[2], cb], axis=-1)
        a = a.reshape(NJ, 128, 4).transpose(1, 0, 2).reshape(128, NJ * 4)
        put("ffn_cw%d" % l, a)
    for e in range(2):
        put("pool_scale%d" % e, _fm(inp["pool_scale"][e]))
        dw = np.asarray(inp["conv_dw_w"][e], f32)
        a = dw.T.reshape(4, 128, 31).transpose(1, 0, 2).reshape(128, 4 * 31)
        put("dw_w%d" % e, a)
        put("dw_b%d" % e, _fm(inp["conv_dw_b"][e]))
        put("ln_g%d" % e, _fm(inp["conv_ln_g"][e]))
        put("ln_b%d" % e, _fm(inp["conv_ln_b"][e]))
        put("g_q%d" % e, _fm(inp["mla_q_norm_g"][e]))
        put("g_kv%d" % e, _fm(inp["mla_kv_norm_g"][e]))
    put("g_final", _fm(inp["final_norm_g"]))
    inv = (1.0 / (np.float32(10000.0) ** (np.arange(0, 32, 2, dtype=f32) / np.float32(32)))).astype(f32)
    p = np.arange(128)
    put("inv128", inv[p % 16][:, None])
    put("sgn128", np.where((p % 32) < 16, -1.0, 1.0).astype(f32)[:, None])
    ic = np.zeros((128, 64), f32)
    for gi, w in enumerate((2, 4, 8, 16)):
        ic[:, gi * 16:(gi + 1) * 16] = 1.0 / np.minimum(np.arange(16) + 1, w).astype(f32)[None, :]
    put("invcnt", ic)

    cbf = np.zeros((128, 1024), f32)
    cbf[:, 0:128] = 1.0
    cbf[:, 128:256] = np.eye(128, dtype=f32)
    cbf[:, 256:384] = (p[:, None] <= p[None, :]).astype(f32)
    for a_ in range(4):
        for i in range(32):
            cbf[32 * a_ + i, 384 + a_ * 96 + i] = 1.0
    for i in range(32):
        cbf[i, 768 + i] = 1.0
    cbf[:, 864:992] = np.where(p[:, None] > p[None, :], -30000.0, 0.0).astype(f32)

    wffn = np.zeros((4, 2, 128, NJH * 3072), f32)
    for l in range(4):
        wu = np.asarray(inp["ffn_w_up"][l], f32).reshape(KC, 128, 2, NJ, 128)
        wd = np.asarray(inp["ffn_w_down"][l], f32).reshape(NJ, 128, D)
        up = wu.transpose(3, 1, 0, 2, 4).reshape(NJ, 128, KC * 256)
        for s in range(2):
            blk = np.concatenate([up[s * NJH:(s + 1) * NJH], wd[s * NJH:(s + 1) * NJH]], axis=2)
            wffn[l, s] = blk.transpose(1, 0, 2).reshape(128, NJH * 3072)
    sh = {"vecs": vecs, "cbf": cbf, "wffn": wffn}
    sh.update(prep_shared2(inp))
    return sh


def _prep_mixe(inp):
    f32 = np.float32
    out = np.zeros((2, 128, 20992), f32)
    for e in range(2):
        win = np.asarray(inp["pc_w_in"][e], f32).reshape(KC, 128, 1536).transpose(1, 0, 2).reshape(128, KC * 1536)
        pw = np.asarray(inp["pool_w"][e], f32).transpose(1, 0, 2).reshape(128, 512)
        wo = np.asarray(inp["pc_w_out"][e], f32).reshape(KC, 128, D).transpose(1, 0, 2).reshape(128, KC * D)
        out[e] = np.concatenate([win, pw, wo], axis=1)
    return out


def _prep_mla(inp):
    f32 = np.float32
    out = np.zeros((2, 128, 27136), f32)
    swp = (np.arange(32) + 16) % 32
    for o in range(2):
        wd = np.asarray(inp["mla_w_dq_dkv"][o], f32)
        A = np.zeros((D, 768), f32)
        A[:, 0:672] = wd
        A[:, 672:704] = wd[:, 640:672][:, swp]
        wq = np.asarray(inp["mla_w_uq"][o], f32).reshape(384, 16, 96)
        B = np.zeros((384, 2560), f32)
        pe = wq[:, :, 64:96]
        B[:, 0:512] = pe.reshape(384, 512)
        B[:, 512:1024] = pe[:, :, swp].reshape(384, 512)
        Bh = np.zeros((384, 16, 96), f32)
        Bh[:, :, 32:96] = wq[:, :, 0:64]
        B[:, 1024:2560] = Bh.reshape(384, 1536)
        wk = np.asarray(inp["mla_w_ukv"][o], f32).reshape(256, 16, 128)
        C = np.zeros((256, 2560), f32)
        Ch = np.zeros((256, 16, 96), f32)
        Ch[:, :, 32:96] = wk[:, :, 0:64]
        C[:, 0:1536] = Ch.reshape(256, 1536)
        C[:, 1536:2560] = wk[:, :, 64:128].reshape(256, 1024)
        wo = np.asarray(inp["mla_w_o"][o], f32)
        fm = lambda w: w.reshape(-1, 128, w.shape[1]).transpose(1, 0, 2).reshape(128, -1)
        out[o] = np.concatenate([fm(A), fm(B), fm(C), fm(wo)], axis=1)
    return out


def prep_shared2(inp):
    f32 = np.float32
    wxa = np.zeros((4, 128, 16384), f32)
    wkv = np.zeros((4, 128, 16384), f32)
    for l in range(4):
        wq = np.asarray(inp["xa_wq"][l], f32).reshape(KC, 128, D).transpose(1, 0, 2).reshape(128, KC * D)
        wo = np.asarray(inp["xa_wo"][l], f32).reshape(KC, 128, D).transpose(1, 0, 2).reshape(128, KC * D)
        wxa[l] = np.concatenate([wq, wo], axis=1)
        wkv[l] = np.asarray(inp["xa_wkv"][l], f32).reshape(KC, 128, 2 * D).transpose(1, 0, 2).reshape(128, KC * 2 * D)
    return {"wxa": wxa, "wkv": wkv,
            "wmixe": _prep_mixe(inp), "wmla": _prep_mla(inp)}


ALL_PHASES = []


def run_prog(prog, shared, xT_list, memT_list, pos_list):
    in_maps = []
    for c in range(len(xT_list)):
        m = dict(shared)
        m["xT"] = xT_list[c]
        m["memT"] = memT_list[c]
        m["pos"] = pos_list[c]
        in_maps.append(m)
    res = run_bass_kernel_spmd(prog.nc, in_maps, core_ids=list(range(len(xT_list))))
    return [r["yT"] for r in res.results]


def all_phases():
    ph = [("rope",)] + [("memkv", l) for l in range(4)]
    for l in range(4):
        if l % 2 == 0:
            ph.append(("mixe", l))
        else:
            ph += [("mla1", l), ("mla2", l), ("mla3", l)]
        ph += [("xa", l), ("ffn", l, 0), ("ffn", l, 1)]
    ph.append(("final",))
    return ph


def kernel(**inputs):
    x = np.asarray(inputs["x"], np.float32)
    mem = np.asarray(inputs["mem"], np.float32)
    positions = np.asarray(inputs["positions"]).astype(np.int32)
    B, T, _ = x.shape
    NT = T // TT
    shared = prep_shared(inputs)
    prog = Prog(NT, all_phases())
    xT = [np.ascontiguousarray(x[b].T) for b in range(B)]
    memT = [np.ascontiguousarray(mem[b].T) for b in range(B)]
    pos = [np.ascontiguousarray(np.broadcast_to(positions[b][None, :], (128, T))) for b in range(B)]
    ys = run_prog(prog, shared, xT, memT, pos)
    return np.stack([np.ascontiguousarray(y.T) for y in ys]).astype(np.float32)
```
